# Optimizing a Trainium2 kernel written in Bass

```python
import math
import jax
import jax.numpy as jnp
from jax import lax
import numpy as np


D_MODEL = 1024
BATCH = 4
SEQ = 8192
DEPTH = 4

CTX_LEN = 256
GRID_W = 64

RWKV_HEADS = 8
RWKV_HEAD_DIM = 64
RWKV_W = RWKV_HEADS * RWKV_HEAD_DIM
DECAY_LORA = 64
ICLR_LORA = 64
GATE_LORA = 128
RWKV_COLS = 3 * RWKV_W + DECAY_LORA + ICLR_LORA + GATE_LORA
N_DIR = 2

DIFF_HEADS = 8
DIFF_HEAD_DIM = 32
DIFF_V_DIM = 2 * DIFF_HEAD_DIM
DIFF_QK = DIFF_HEADS * 2 * DIFF_HEAD_DIM
DIFF_W = DIFF_HEADS * DIFF_V_DIM
DIFF_COLS = 2 * DIFF_QK + DIFF_W

IN_COLS = RWKV_COLS + DIFF_COLS
MIX_W = RWKV_W + DIFF_W
D_FF = 4 * D_MODEL
Q_BLOCK = 128
ROPE_BASE = 10000.0
NORM_EPS = 1e-6
LNX_EPS = 64e-5
SUBLN_EPS = 1e-5

kernel_name = "hybrid_rwkv7_diffattn_dit"


def rms_norm(x, g, eps=NORM_EPS):
    x32 = x.astype(jnp.float32)
    y = x32 * lax.rsqrt(jnp.mean(x32 * x32, axis=-1, keepdims=True) + eps)
    return (y * g.astype(jnp.float32)).astype(x.dtype)


def modulate(h, shift, scale):
    return h * (1 + scale) + shift


def bi_token_shift(z, mu):
    zp = jnp.pad(z, ((0, 0), (1, 1), (0, 0)))
    return z + mu * (0.5 * (zp[:, :-2] + zp[:, 2:]) - z)


def rope_1d(x, pos):
    n = x.shape[-1] // 2
    inv_freq = ROPE_BASE ** (-jnp.arange(n, dtype=jnp.float32) / n)
    ang = pos.astype(jnp.float32)[:, None] * inv_freq[None, :]
    cos = jnp.cos(ang)[None, :, None, :]
    sin = jnp.sin(ang)[None, :, None, :]
    x1 = x[..., :n].astype(jnp.float32)
    x2 = x[..., n:].astype(jnp.float32)
    return jnp.concatenate([x1 * cos - x2 * sin, x1 * sin + x2 * cos], axis=-1).astype(x.dtype)


def axial_rope(x, rows, cols):
    half = x.shape[-1] // 2
    return jnp.concatenate([rope_1d(x[..., :half], rows), rope_1d(x[..., half:], cols)], axis=-1)


def diff_attend(q1, q2, k1, k2, v, lam):
    bsz, lq, nh, dh = q1.shape
    nb = lq // Q_BLOCK
    scale = dh ** -0.5

    def blocks(q):
        return q.reshape(bsz, nb, Q_BLOCK, nh, dh).transpose(1, 0, 2, 3, 4)

    def one_block(qs):
        a1, a2 = qs
        s1 = jnp.einsum("bqhd,bkhd->bhqk", a1, k1).astype(jnp.float32) * scale
        s2 = jnp.einsum("bqhd,bkhd->bhqk", a2, k2).astype(jnp.float32) * scale
        p = jax.nn.softmax(s1, axis=-1) - lam * jax.nn.softmax(s2, axis=-1)
        return jnp.einsum("bhqk,bkhd->bqhd", p.astype(v.dtype), v)

    o = lax.map(one_block, (blocks(q1), blocks(q2)))
    return o.transpose(1, 0, 2, 3, 4).reshape(bsz, lq, nh, v.shape[-1])


def wkv_scan(state0, r, w, k, v, kk, b, reverse):
    xs = tuple(jnp.swapaxes(t.astype(jnp.float32), 0, 1) for t in (r, w, k, v, kk, b))

    def step(state, inp):
        r_t, w_t, k_t, v_t, kk_t, b_t = inp
        sa = jnp.einsum("bhvk,bhk->bhv", state, kk_t)
        state = (state * w_t[:, :, None, :]
                 - sa[..., :, None] * b_t[:, :, None, :]
                 + v_t[..., :, None] * k_t[:, :, None, :])
        y = jnp.einsum("bhvk,bhk->bhv", state, r_t)
        return state, y

    state, ys = lax.scan(step, state0, xs, reverse=reverse)
    return state, jnp.swapaxes(ys, 0, 1)


def rwkv_inputs(z, k_k):
    bsz, t, _ = z.shape
    heads = lambda u: u.reshape(bsz, t, RWKV_HEADS, RWKV_HEAD_DIM)
    r = heads(z[..., :RWKV_W])
    k = heads(z[..., RWKV_W:2 * RWKV_W])
    v = heads(z[..., 2 * RWKV_W:3 * RWKV_W])
    o = 3 * RWKV_W
    xw = z[..., o:o + DECAY_LORA]
    xa = z[..., o + DECAY_LORA:o + DECAY_LORA + ICLR_LORA]
    xg = z[..., o + DECAY_LORA + ICLR_LORA:]
    kk = (k * k_k.reshape(RWKV_HEADS, RWKV_HEAD_DIM)).astype(jnp.float32)
    kk = kk / jnp.maximum(jnp.sqrt(jnp.sum(kk * kk, axis=-1, keepdims=True)), 1e-12)
    return r, k, v, kk, xw, xa, xg


def direction_terms(k, kk, xw, xa, w0, w_b, a0, a_b, k_a):
    shp = k.shape
    wl = (w0 + jnp.tanh(xw) @ w_b).astype(jnp.float32)
    decay = jnp.exp(-jnp.exp(-jax.nn.softplus(-wl) - 0.5)).reshape(shp)
    a = jax.nn.sigmoid((a0 + xa @ a_b).astype(jnp.float32)).reshape(shp)
    k_a32 = k_a.reshape(RWKV_HEADS, RWKV_HEAD_DIM).astype(jnp.float32)
    k_mod = k.astype(jnp.float32) * (1 + (a - 1) * k_a32)
    return decay, k_mod, kk * a


def rwkv_finish(y, r, k, v, xg, r_k, g_b, lnx_g, lnx_b, dtype):
    bsz, t = y.shape[:2]
    mu = jnp.mean(y, axis=-1, keepdims=True)
    var = jnp.mean(jnp.square(y - mu), axis=-1, keepdims=True)
    yn = ((y - mu) * lax.rsqrt(var + LNX_EPS)).reshape(bsz, t, RWKV_W)
    yn = yn * lnx_g.astype(jnp.float32) + lnx_b.astype(jnp.float32)
    r32, k32, v32 = (u.astype(jnp.float32) for u in (r, k, v))
    bonus = (jnp.sum(r32 * k32 * r_k.astype(jnp.float32), axis=-1, keepdims=True) * v32).reshape(bsz, t, RWKV_W)
    gate = (jax.nn.sigmoid(xg) @ g_b).astype(jnp.float32)
    return ((yn + bonus) * gate).astype(dtype)


def rwkv_group(z, zc, k_k, k_a, w0, w_b, a0, a_b, g_b, r_k, lnx_g, lnx_b, need_ctx):
    r, k, v, kk, xw, xa, xg = rwkv_inputs(z, k_k)
    rc, kc, vc, kkc, xwc, xac, xgc = rwkv_inputs(zc, k_k)
    state0 = jnp.zeros((z.shape[0], RWKV_HEADS, RWKV_HEAD_DIM, RWKV_HEAD_DIM), jnp.float32)
    y_lat = jnp.zeros(r.shape, jnp.float32)
    y_ctx = jnp.zeros(rc.shape, jnp.float32)
    for d, reverse in enumerate((False, True)):
        dec_c, km_c, b_c = direction_terms(kc, kkc, xwc, xac, w0[d], w_b[d], a0[d], a_b[d], k_a)
        dec, km, bb = direction_terms(k, kk, xw, xa, w0[d], w_b[d], a0[d], a_b[d], k_a)
        state_c, yc = wkv_scan(state0, rc, dec_c, km_c, vc, kkc, b_c, reverse)
        _, yl = wkv_scan(state_c, r, dec, km, v, kk, bb, reverse)
        y_lat = y_lat + yl
        y_ctx = y_ctx + yc
    out = rwkv_finish(y_lat, r, k, v, xg, r_k, g_b, lnx_g, lnx_b, z.dtype)
    out_c = rwkv_finish(y_ctx, rc, kc, vc, xgc, r_k, g_b, lnx_g, lnx_b, z.dtype) if need_ctx else None
    return out, out_c


def diff_split(t):
    bsz, n, _ = t.shape
    q = t[..., :DIFF_QK].reshape(bsz, n, DIFF_HEADS, 2, DIFF_HEAD_DIM)
    k = t[..., DIFF_QK:2 * DIFF_QK].reshape(bsz, n, DIFF_HEADS, 2, DIFF_HEAD_DIM)
    v = t[..., 2 * DIFF_QK:].reshape(bsz, n, DIFF_HEADS, DIFF_V_DIM)
    return q[..., 0, :], q[..., 1, :], k[..., 0, :], k[..., 1, :], v


def diff_group(z, zc, lam_q1, lam_k1, lam_q2, lam_k2, subln_g, lambda_init, rows, cols, need_ctx):
    q1, q2, k1, k2, v = diff_split(z)
    q1c, q2c, k1c, k2c, vc = diff_split(zc)
    q1, q2, k1, k2 = (axial_rope(u, rows, cols) for u in (q1, q2, k1, k2))
    lam = (jnp.exp(jnp.sum(lam_q1.astype(jnp.float32) * lam_k1.astype(jnp.float32)))
           - jnp.exp(jnp.sum(lam_q2.astype(jnp.float32) * lam_k2.astype(jnp.float32)))
           + lambda_init)
    kk1 = jnp.concatenate([k1, k1c], axis=1)
    kk2 = jnp.concatenate([k2, k2c], axis=1)
    vv = jnp.concatenate([v, vc], axis=1)
    bsz, n = z.shape[:2]
    o = diff_attend(q1, q2, kk1, kk2, vv, lam)
    out = (rms_norm(o, subln_g, SUBLN_EPS) * (1 - lambda_init)).reshape(bsz, n, DIFF_W)
    out_c = None
    if need_ctx:
        oc = diff_attend(q1c, q2c, k1c, k2c, vc, lam)
        out_c = (rms_norm(oc, subln_g, SUBLN_EPS) * (1 - lambda_init)).reshape(bsz, zc.shape[1], DIFF_W)
    return out, out_c


def sq_relu_mlp(h, w1, w2):
    return jnp.square(jax.nn.relu(h @ w1)) @ w2


def setup_inputs(seed: int = 0) -> dict:
    key = jax.random.key(seed)
    ks = jax.random.split(key, 32)
    f32 = jnp.float32
    D = D_MODEL
    nrm = lambda k, shape, s: jax.random.normal(k, shape, f32) * s
    return {
        "x": nrm(ks[0], (BATCH, SEQ, D), 1.0),
        "c": nrm(ks[1], (BATCH, D), 1.0),
        "ctx": nrm(ks[2], (BATCH, CTX_LEN, D), 1.0),
        "c_ctx": nrm(ks[3], (D,), 1.0),
        "ada_w": nrm(ks[4], (DEPTH, D, 6 * D), 0.5 * D ** -0.5),
        "ada_b": nrm(ks[5], (DEPTH, 6 * D), 0.02),
        "g_pre_mix": 1.0 + nrm(ks[6], (DEPTH, D), 0.02),
        "g_post_mix": 1.0 + nrm(ks[7], (DEPTH, D), 0.02),
        "g_pre_mlp": 1.0 + nrm(ks[8], (DEPTH, D), 0.02),
        "g_post_mlp": 1.0 + nrm(ks[9], (DEPTH, D), 0.02),
        "w_in": nrm(ks[10], (DEPTH, D, IN_COLS), D ** -0.5),
        "shift_mu": jax.random.uniform(ks[11], (DEPTH, RWKV_COLS), f32),
        "k_k": 0.85 + nrm(ks[12], (DEPTH, RWKV_W), 0.02),
        "k_a": 1.0 + nrm(ks[13], (DEPTH, RWKV_W), 0.02),
        "w0": jax.random.uniform(ks[14], (DEPTH, N_DIR, RWKV_W), f32, -3.0, 0.0),
        "w_b": nrm(ks[15], (DEPTH, N_DIR, DECAY_LORA, RWKV_W), 0.1),
        "a0": nrm(ks[16], (DEPTH, N_DIR, RWKV_W), 0.1),
        "a_b": nrm(ks[17], (DEPTH, N_DIR, ICLR_LORA, RWKV_W), 0.1),
        "g_b": nrm(ks[18], (DEPTH, GATE_LORA, RWKV_W), GATE_LORA ** -0.5),
        "r_k": nrm(ks[19], (DEPTH, RWKV_HEADS, RWKV_HEAD_DIM), 0.1),
        "lnx_g": 1.0 + nrm(ks[20], (DEPTH, RWKV_W), 0.02),
        "lnx_b": nrm(ks[21], (DEPTH, RWKV_W), 0.02),
        "lam_q1": nrm(ks[22], (DEPTH, DIFF_HEAD_DIM), 0.1),
        "lam_k1": nrm(ks[23], (DEPTH, DIFF_HEAD_DIM), 0.1),
        "lam_q2": nrm(ks[24], (DEPTH, DIFF_HEAD_DIM), 0.1),
        "lam_k2": nrm(ks[25], (DEPTH, DIFF_HEAD_DIM), 0.1),
        "subln_g": 1.0 + nrm(ks[26], (DEPTH, DIFF_V_DIM), 0.02),
        "w_out": nrm(ks[27], (DEPTH, MIX_W, D), MIX_W ** -0.5),
        "w_ff1": nrm(ks[28], (DEPTH, D, D_FF), D ** -0.5),
        "w_ff2": nrm(ks[29], (DEPTH, D_FF, D), D_FF ** -0.5),
    }


def reference(x, c, ctx, c_ctx, ada_w, ada_b, g_pre_mix, g_post_mix, g_pre_mlp, g_post_mlp,
              w_in, shift_mu, k_k, k_a, w0, w_b, a0, a_b, g_b, r_k, lnx_g, lnx_b,
              lam_q1, lam_k1, lam_q2, lam_k2, subln_g, w_out, w_ff1, w_ff2):
    n_tok = x.shape[1]
    n_rows = n_tok // GRID_W
    rows = jnp.repeat(jnp.arange(n_rows, dtype=jnp.int32), GRID_W)
    cols = jnp.tile(jnp.arange(GRID_W, dtype=jnp.int32), n_rows)
    sc = jax.nn.silu(c)
    scc = jax.nn.silu(c_ctx)
    xc = ctx
    for l in range(DEPTH):
        need_ctx = l < DEPTH - 1
        lambda_init = 0.8 - 0.6 * math.exp(-0.3 * l)
        mod = (sc @ ada_w[l] + ada_b[l])[:, None, :]
        modc = scc @ ada_w[l] + ada_b[l]
        sh1, s1, g1, sh2, s2, g2 = jnp.split(mod, 6, axis=-1)
        sh1c, s1c, g1c, sh2c, s2c, g2c = jnp.split(modc, 6, axis=-1)

        h = modulate(rms_norm(x, g_pre_mix[l]), sh1, s1)
        hc = modulate(rms_norm(xc, g_pre_mix[l]), sh1c, s1c)
        z = h @ w_in[l]
        zc = hc @ w_in[l]
        zr = bi_token_shift(z[..., :RWKV_COLS], shift_mu[l])
        zrc = bi_token_shift(zc[..., :RWKV_COLS], shift_mu[l])
        o_r, o_rc = rwkv_group(zr, zrc, k_k[l], k_a[l], w0[l], w_b[l], a0[l], a_b[l], g_b[l],
                               r_k[l], lnx_g[l], lnx_b[l], need_ctx)
        o_d, o_dc = diff_group(z[..., RWKV_COLS:], zc[..., RWKV_COLS:], lam_q1[l], lam_k1[l],
                               lam_q2[l], lam_k2[l], subln_g[l], lambda_init, rows, cols, need_ctx)
        o = jnp.concatenate([o_r, o_d], axis=-1) @ w_out[l]
        x = x + g1 * rms_norm(o, g_post_mix[l])
        if need_ctx:
            oc = jnp.concatenate([o_rc, o_dc], axis=-1) @ w_out[l]
            xc = xc + g1c * rms_norm(oc, g_post_mix[l])

        f = sq_relu_mlp(modulate(rms_norm(x, g_pre_mlp[l]), sh2, s2), w_ff1[l], w_ff2[l])
        x = x + g2 * rms_norm(f, g_post_mlp[l])
        if need_ctx:
            fc = sq_relu_mlp(modulate(rms_norm(xc, g_pre_mlp[l]), sh2c, s2c), w_ff1[l], w_ff2[l])
            xc = xc + g2c * rms_norm(fc, g_post_mlp[l])
    return x
```

```python
import math
from contextlib import ExitStack
import numpy as np
import ml_dtypes
import concourse.bass as bass
import concourse.mybir as mybir
from concourse.bass_utils import run_bass_kernel_spmd

F32 = mybir.dt.float32
BF16 = mybir.dt.bfloat16
ALU = mybir.AluOpType
AF = mybir.ActivationFunctionType
AX = mybir.AxisListType

D = 1024
CTX = 256
GRID_W = 64
NDS = 40
ENG = ["pe", "dve", "act", "pool", "sp"]


class Buf:
    __slots__ = ("lw", "rd", "name", "excl")

    def __init__(self, name="", excl=False):
        self.lw = None
        self.rd = {}
        self.name = name
        self.excl = excl


class V:
    def __init__(self, ap, buf):
        self.ap = ap
        self.buf = buf

    def __getitem__(self, k):
        return V(self.ap[k], self.buf)

    def re(self, s, **kw):
        return V(self.ap.rearrange(s, **kw), self.buf)

    def bc(self, shape):
        return V(self.ap.broadcast_to(shape), self.buf)

    def un(self, ax):
        return V(self.ap.unsqueeze(ax), self.buf)

    def bitcast(self, dt):
        return V(self.ap.bitcast(dt), self.buf)


class Sched:
    def __init__(self, nc, es):
        self.nc = nc
        self.es = es
        self.prog = {e: [] for e in ENG}
        self.sem = {e: es.enter_context(nc.semaphore("c_" + e)) for e in ENG}
        self.cnt = {e: 0 for e in ENG}
        self.seen = {e: {} for e in ENG}
        self.dsem = [es.enter_context(nc.semaphore("d%d" % i)) for i in range(NDS)]
        self.dval = [0] * NDS
        self.dnext = 0
        self.ninst = 0

    def _wait(self, eng, key, n):
        if key[0] == "e" and key[1] == eng and eng == "pe":
            return
        if self.seen[eng].get(key, 0) >= n:
            return
        self.seen[eng][key] = n
        sem = self.sem[key[1]] if key[0] == "e" else self.dsem[key[1]]
        self.prog[eng].append(("w", sem, n))

    def _deps(self, eng, reads, writes):
        for b in reads:
            if b.lw is not None:
                self._wait(eng, b.lw[0], b.lw[1])
            if b.excl:
                for k, n in b.rd.items():
                    if k != ("e", eng):
                        self._wait(eng, k, n)
        for b in writes:
            if b.lw is not None:
                self._wait(eng, b.lw[0], b.lw[1])
            for k, n in b.rd.items():
                self._wait(eng, k, n)

    def _mark(self, t, reads, writes):
        for b in reads:
            if b.rd.get(t[0], 0) < t[1]:
                b.rd[t[0]] = t[1]
        for b in writes:
            b.lw = t
            b.rd = {}

    def op(self, eng, fn, reads=(), writes=()):
        self._deps(eng, reads, writes)
        self.cnt[eng] += 1
        t = (("e", eng), self.cnt[eng])
        self.prog[eng].append(("i", fn))
        self._mark(t, reads, writes)
        self.ninst += 1
        return t

    def dma(self, q, out, in_, reads=(), writes=()):
        idx = self.dnext
        self.dnext = (self.dnext + 1) % NDS
        if self.dval[idx] > 0:
            self._wait(q, ("d", idx), self.dval[idx])
        self._deps(q, reads, writes)
        self.dval[idx] += 16
        t = (("d", idx), self.dval[idx])
        self.prog[q].append(("d", out, in_, self.dsem[idx]))
        self._mark(t, reads, writes)
        self.ninst += 1
        return t

    def barrier(self):
        for e in ENG:
            for e2 in ENG:
                if e2 != e and self.cnt[e2] > 0:
                    self._wait(e, ("e", e2), self.cnt[e2])
            for i in range(NDS):
                if self.dval[i] > 0:
                    self._wait(e, ("d", i), self.dval[i])

    def emit(self, block):
        names = dict(pe="tensor", dve="vector", act="scalar", pool="gpsimd", sp="sync")
        for e in ENG:
            prog = self.prog[e]
            sem = self.sem[e]

            def body(h, prog=prog, sem=sem):
                for it in prog:
                    if it[0] == "w":
                        h.wait_ge(it[1], it[2])
                    elif it[0] == "i":
                        it[1](h).then_inc(sem, 1)
                    else:
                        h.dma_start(out=it[1], in_=it[2]).then_inc(it[3], 16)

            getattr(block, names[e])(body)


def _bufs(*vs):
    return [v.buf for v in vs if isinstance(v, V)]


def _a(x):
    return x.ap if isinstance(x, V) else x


class K:
    def __init__(self, S):
        self.S = S

    def tt(self, eng, out, a, b, op):
        return self.S.op(eng, lambda h, o=out.ap, x=a.ap, y=b.ap: h.tensor_tensor(out=o, in0=x, in1=y, op=op),
                         _bufs(a, b), _bufs(out))

    def ts(self, eng, out, a, s1, s2, op0, op1=None, accum=None):
        kw = {}
        if op1 is not None:
            kw["op1"] = op1
        if accum is not None:
            kw["accum_out"] = accum.ap
        return self.S.op(eng, lambda h, o=out.ap, x=a.ap, p=_a(s1), q=_a(s2): h.tensor_scalar(
            out=o, in0=x, scalar1=p, scalar2=q, op0=op0, **kw), _bufs(a, s1, s2), _bufs(out, accum))

    def stt(self, out, a, s, b, op0, op1):
        return self.S.op("dve", lambda h, o=out.ap, x=a.ap, p=_a(s), y=b.ap: h.scalar_tensor_tensor(
            out=o, in0=x, scalar=p, in1=y, op0=op0, op1=op1), _bufs(a, s, b), _bufs(out))

    def act(self, out, a, func, scale=1.0, bias=None, accum=None):
        kw = {}
        if bias is not None:
            kw["bias"] = _a(bias)
        if accum is not None:
            kw["accum_out"] = accum.ap
        return self.S.op("act", lambda h, o=out.ap, x=a.ap, sc=_a(scale): h.activation(
            out=o, in_=x, func=func, scale=sc, **kw), _bufs(a, scale, bias), _bufs(out, accum))

    def cp(self, eng, out, a):
        if eng == "act":
            return self.act(out, a, AF.Copy)
        return self.S.op(eng, lambda h, o=out.ap, x=a.ap: h.tensor_copy(out=o, in_=x), _bufs(a), _bufs(out))

    def red(self, out, a, op, axis=AX.X):
        return self.S.op("dve", lambda h, o=out.ap, x=a.ap: h.tensor_reduce(out=o, in_=x, axis=axis, op=op),
                         _bufs(a), _bufs(out))

    def recip(self, out, a):
        return self.S.op("dve", lambda h, o=out.ap, x=a.ap: h.reciprocal(out=o, in_=x), _bufs(a), _bufs(out))

    def memset(self, eng, out, val):
        return self.S.op(eng, lambda h, o=out.ap: h.memset(o, val), [], _bufs(out))

    def mm(self, out, lhsT, rhs, start=True, stop=True):
        return self.S.op("pe", lambda h, o=out.ap, l=lhsT.ap, r=rhs.ap: h.matmul(o, l, r, start=start, stop=stop),
                         _bufs(lhsT, rhs), _bufs(out))

    def tr(self, out, a, ident):
        return self.S.op("pe", lambda h, o=out.ap, x=a.ap, i=ident.ap: h.transpose(o, x, i),
                         _bufs(a, ident), _bufs(out))

    def dma(self, q, out, in_):
        return self.S.dma(q, _a(out), _a(in_), _bufs(in_), _bufs(out))


def build(cfg):
    L = cfg["L"]
    DEPTH = cfg["DEPTH"]
    HL = cfg.get("HL", 8)
    dbg = cfg.get("dbg", ())
    phases = cfg.get("phases", "MABCDE")
    T = CTX + L
    NT = T // 128
    NCT = CTX // 128
    HW = 64 * HL
    RC = 3 * HW + 256
    ZC = RC + 3 * HW
    MIXW = 2 * HW

    nc = bass.Bass("TRN2", target_bir_lowering=False)

    def din(name, shape, dt=F32):
        return nc.dram_tensor(name, list(shape), dt, kind="ExternalInput").ap()

    def dint(name, shape, dt=F32):
        kind = "ExternalOutput" if name in dbg else "Internal"
        return nc.dram_tensor(name, list(shape), dt, kind=kind).ap()

    xin = din("xin", [T, D])
    c2 = din("c2", [2, D])
    ada_w = din("ada_w", [DEPTH, D, 6 * D])
    ada_b = din("ada_b", [DEPTH, 6 * D])
    gvec = {n: din(n, [DEPTH, D]) for n in ("g_pre_mix", "g_post_mix", "g_pre_mlp", "g_post_mlp")}
    w_in = din("w_in", [DEPTH, D, ZC])
    shift_mu = din("shift_mu", [DEPTH, RC])
    k_k = din("k_k", [DEPTH, HW])
    k_a = din("k_a", [DEPTH, HW])
    w0 = din("w0", [DEPTH, 2, HW])
    w_b = din("w_b", [DEPTH, 2, 64, HW])
    a0 = din("a0", [DEPTH, 2, HW])
    a_b = din("a_b", [DEPTH, 2, 64, HW])
    g_b = din("g_b", [DEPTH, 128, HW])
    r_k = din("r_k", [DEPTH, HW])
    lnx_g = din("lnx_g", [DEPTH, HW])
    lnx_b = din("lnx_b", [DEPTH, HW])
    lamv = {n: din(n, [DEPTH, 32]) for n in ("lam_q1", "lam_k1", "lam_q2", "lam_k2")}
    subln_g = din("subln_g", [DEPTH, 64])
    w_out = din("w_out", [DEPTH, MIXW, D])
    w_ff1 = din("w_ff1", [DEPTH, D, 4 * D])
    w_ff2 = din("w_ff2", [DEPTH, 4 * D, D])
    c_ident = din("c_ident", [128, 128])
    c_masks = din("c_masks", [128, 512])
    c_masks2 = din("c_masks2", [128, 1024])
    c_rope = din("c_rope", [T, 64])

    out = nc.dram_tensor("out", [L, D], F32, kind="ExternalOutput").ap()
    xs = dint("xs", [T, D])
    modd = dint("modd", [2, 6 * D])
    zr = dint("zr", [T + 3, RC])
    qkT = dint("qkT", [2 * HW, T], BF16)
    vv = dint("vv", [T, HL * 65], BF16)
    yd = dint("yd", [2, T, HW])
    bgd = dint("bgd", [T, 2 * HW])
    omix = dint("omix", [T, MIXW], BF16)
    nbd = dint("nbd", [1, 2 * HL])

    def zrow(ti):
        return 1 + ti * 128 if ti < NCT else CTX + 2 + (ti - NCT) * 128

    es = ExitStack()
    with es:
        S = Sched(nc, es)
        k = K(S)

        uid = [0]

        def sb(name, shape, dt=F32, stack=None):
            uid[0] += 1
            name = "%s_%d" % (name, uid[0])
            t = (stack or es).enter_context(nc.sbuf_tensor(name, list(shape), dt))
            return V(t[:] if len(shape) == 2 else t[:], Buf(name))

        psb = []
        for i in range(8):
            t = es.enter_context(nc.psum_tensor("ps%d" % i, [128, 512], F32))
            psb.append(V(t[:], Buf("ps%d" % i, True)))
        psn = [0]

        def ps():
            v = psb[psn[0] % 8]
            psn[0] += 1
            return v

        ident = sb("ident", [128, 128])
        identb = sb("identb", [128, 128], BF16)
        ones = sb("ones", [128, 128])
        sc = sb("sc", [128, 8, 2])
        sctmp = sb("sctmp", [128, 2, 8])
        st0 = ExitStack()
        zrow_sb = sb("zrow_sb", [1, RC], F32, st0)
        k.dma("sp", ident, c_ident)
        k.cp("dve", identb, ident)
        k.memset("dve", ones, 1.0)
        k.memset("dve", zrow_sb, 0.0)
        for r in range(2):
            k.dma("sp", sctmp[:, r, :], c2[r].rearrange("(p kc) -> p kc", kc=8))
        k.act(sc.re("p kc r -> p r kc"), sctmp, AF.Silu)
        for row in (0, CTX + 1, T + 2):
            k.dma("sp", zr[row:row + 1, :], zrow_sb)
        S.barrier()
        st0.close()
        HM = [sb("hmask%d" % i, [128, 128]) for i in range(8)]
        for i, m_ in enumerate(HM):
            k.dma("sp", m_, c_masks2[:, i * 128:(i + 1) * 128])
        MU, MUI, ML, MLI = (sb("mask%d" % i, [128, 128]) for i in range(4))
        for i, m_ in enumerate((MU, MUI, ML, MLI)):
            k.dma("sp", m_, c_masks[:, i * 128:(i + 1) * 128])
        S.barrier()

        def rstd_of(ss, n, eps, tmp):
            k.ts("dve", tmp, ss, 1.0 / n, eps, ALU.mult, ALU.add)
            k.act(tmp, tmp, AF.Sqrt)
            k.recip(ss, tmp)

        def load_bcast(dst, src_row):
            k.dma("sp", dst, src_row.broadcast_to([128, src_row.shape[-1]]))

        def load_w_bf16(stack, name, src3, nk, ncols, stg):
            w = sb(name, [128, nk, ncols], BF16, stack)
            engs = ["pool", "act", "dve"]
            i = 0
            for kc in range(nk):
                sw = stg[0].ap.shape[1]
                for n0 in range(0, ncols, sw):
                    n1 = min(ncols, n0 + sw)
                    s = stg[i % len(stg)]
                    k.dma("sp", s[:, 0:n1 - n0], src3[:, kc, n0:n1])
                    k.cp(engs[i % 3], w[:, kc, n0:n1], s[:, 0:n1 - n0])
                    i += 1
            return w

        def transposes_bf16(dstT, src, nchunks):
            for c0 in range(0, nchunks, 8):
                cn = min(8, nchunks - c0)
                p = ps()
                pb = p.bitcast(BF16)
                for j in range(cn):
                    k.tr(pb[:, j * 128:(j + 1) * 128], src[:, (c0 + j) * 128:(c0 + j + 1) * 128], identb)
                k.cp("act", dstT[:, c0:c0 + cn, :], pb[:, 0:cn * 128].re("p (c t) -> p c t", t=128))

        def x_src(l):
            return xin if l == 0 else xs

        for l in range(DEPTH):
            need_ctx = l < DEPTH - 1
            lam_init = 0.8 - 0.6 * math.exp(-0.3 * l)
            if "M" in phases:
                with ExitStack() as st:
                    adab = sb("adab", [2, 6 * D], F32, st)
                    modsb = sb("modsb", [2, 6 * D], F32, st)
                    wch = [sb("wch%d" % i, [128, 8, 512], F32, st) for i in range(2)]
                    k.dma("sp", adab, ada_b[l:l + 1, :].broadcast_to([2, 6 * D]))
                    aw = ada_w[l].rearrange("(p kc) n -> p kc n", kc=8)
                    for ci in range(12):
                        w = wch[ci % 2]
                        k.dma("sp", w, aw[:, :, ci * 512:(ci + 1) * 512])
                        p = ps()
                        for kc in range(8):
                            k.mm(p[0:2, :], sc[:, kc, :], w[:, kc, :], start=(kc == 0), stop=(kc == 7))
                        k.tt("dve", modsb[:, ci * 512:(ci + 1) * 512], p[0:2, :], adab[:, ci * 512:(ci + 1) * 512],
                             ALU.add)
                    k.dma("sp", modd, modsb)
                    S.barrier()

            gtmp = [None]

            def mod_tile(stack, name, row, seg, gname=None, plus1=False):
                t = sb(name, [128, D], F32, stack)
                load_bcast(t, modd[row:row + 1, seg * D:(seg + 1) * D])
                if gname is not None:
                    g = gtmp[0]
                    load_bcast(g, gvec[gname][l:l + 1, :])
                    if plus1:
                        k.stt(t, t, 1.0, g, ALU.add, ALU.mult)
                    else:
                        k.tt("dve", t, t, g, ALU.mult)
                return t

            if "A" in phases:
                with ExitStack() as st:
                    stg = [sb("stgA%d" % i, [128, 2048], F32, st) for i in range(2)]

                    wsb = load_w_bf16(st, "w_in_sb", w_in[l].rearrange("(kc p) n -> p kc n", p=128), 8, ZC, stg)
                    tmpA = sb("tmpA", [128, D], F32, st)
                    gtmp[0] = tmpA
                    gs = [mod_tile(st, "gsA%d" % r, r, 1, "g_pre_mix", True) for r in range(2)]
                    sh = [mod_tile(st, "shA%d" % r, r, 0) for r in range(2)]
                    xt = [sb("xtA%d" % i, [128, D], F32, st) for i in range(2)]
                    hb = [sb("hbA%d" % i, [128, D], BF16, st) for i in range(2)]
                    hT = [sb("hTA%d" % i, [128, 8, 128], BF16, st) for i in range(2)]
                    zst = [sb("zstA%d" % i, [128, RC], F32, st) for i in range(2)]
                    qkf = sb("qkfA", [128, 2 * HW], F32, st)
                    qk1 = sb("qk1A", [128, 2 * HW], F32, st)
                    qk2 = sb("qk2A", [128, 2 * HW], F32, st)
                    qkb = sb("qkbA", [128, 2 * HW], BF16, st)
                    qkTs = [sb("qkTsA%d" % i, [128, 2 * HW // 128, 128], BF16, st) for i in range(2)]
                    vst = [sb("vstA%d" % i, [128, HL, 65], BF16, st) for i in range(2)]
                    rp = [sb("rpA%d" % i, [128, 64], F32, st) for i in range(2)]
                    ss = [sb("ssA%d" % i, [128, 1], F32, st) for i in range(2)]
                    sst = sb("sstA", [128, 1], F32, st)
                    nmax = sb("nmaxA", [128, 2 * HL * 2], F32, st)
                    nsq = sb("nsqA", [128, 2 * HL * 2], F32, st)
                    k.memset("dve", nmax, 0.0)
                    for i in range(2):
                        k.memset("pool", vst[i][:, :, 64:65], 1.0)
                    NG = 2 * HW // 32
                    for ti in range(NT):
                        r = 0 if ti >= NCT else 1
                        x = xt[ti % 2]
                        k.dma("sp", x, x_src(l)[ti * 128:(ti + 1) * 128, :])
                        k.dma("sp", rp[ti % 2], c_rope[ti * 128:(ti + 1) * 128, :])
                        s_ = ss[ti % 2]
                        k.act(tmpA, x, AF.Square, accum=s_)
                        rstd_of(s_, D, 1e-6, sst)
                        k.stt(tmpA, x, s_[:, 0:1], gs[r], ALU.mult, ALU.mult)
                        h = hb[ti % 2]
                        k.tt("dve", h, tmpA, sh[r], ALU.add)
                        hTt = hT[ti % 2]
                        transposes_bf16(hTt, h, 8)
                        z = zst[ti % 2]
                        vs_ = vst[ti % 2]
                        for n0 in range(0, ZC, 512):
                            n1 = min(ZC, n0 + 512)
                            p = ps()
                            for kc in range(8):
                                k.mm(p[:, 0:n1 - n0], hTt[:, kc, :], wsb[:, kc, n0:n1], start=(kc == 0), stop=(kc == 7))
                            a0_, a1_ = n0, min(n1, RC)
                            if a1_ > a0_:
                                k.cp("act", z[:, a0_:a1_], p[:, a0_ - n0:a1_ - n0])
                            b0_, b1_ = max(n0, RC), min(n1, RC + 2 * HW)
                            if b1_ > b0_:
                                k.cp("act", qkf[:, b0_ - RC:b1_ - RC], p[:, b0_ - n0:b1_ - n0])
                            c0_, c1_ = max(n0, RC + 2 * HW), n1
                            if c1_ > c0_:
                                hh0 = (c0_ - RC - 2 * HW) // 64
                                hh1 = (c1_ - RC - 2 * HW) // 64
                                k.cp("act", vs_[:, hh0:hh1, 0:64],
                                     p[:, c0_ - n0:c1_ - n0].re("p (h d) -> p h d", d=64))
                        k.dma("pool", zr[zrow(ti):zrow(ti) + 128, :], z)
                        k.dma("pool", vv[ti * 128:(ti + 1) * 128, :], vs_.re("p h d -> p (h d)"))
                        rpt = rp[ti % 2]
                        cosb = rpt[:, 0:32].un(1).bc([128, NG, 32])
                        k.tt("dve", qk1.re("p (g d) -> p g d", d=32), qkf.re("p (g d) -> p g d", d=32), cosb, ALU.mult)
                        x4 = qkf.re("p (g a e) -> p g a e", a=2, e=8)
                        o4 = qk2.re("p (g a e) -> p g a e", a=2, e=8)
                        s4 = rpt[:, 32:64].re("p (c a e) -> p c a e", a=2, e=8)
                        for aa in range(2):
                            for cc in range(2):
                                xin_ = x4[:, cc::2, 1 - aa, :]
                                oo_ = o4[:, cc::2, aa, :]
                                sn_ = s4[:, cc, aa, :].un(1).bc([128, NG, 8])
                                k.tt("pool", oo_, xin_, sn_, ALU.mult)
                        k.tt("dve", qk1, qk1, qk2, ALU.add)
                        k.cp("act", qkb, qk1)
                        k.tt("pool", qk2, qk1, qk1, ALU.mult)
                        k.red(nsq, qk2.re("p (g d) -> p g d", d=32), ALU.add)
                        k.tt("dve", nmax, nmax, nsq, ALU.max)
                        qT = qkTs[ti % 2]
                        transposes_bf16(qT, qkb, 2 * HW // 128)
                        k.dma("pool", qkT[:, ti * 128:(ti + 1) * 128].rearrange("(c p) t -> p c t", p=128), qT)
                    p = ps()
                    k.tr(p[0:2 * HL, 0:128], nmax[:, 0:2 * HL], ident)
                    k.tr(p[0:2 * HL, 128:256], nmax[:, 2 * HL:4 * HL], ident)
                    mq = sb("mqA", [2 * HL, 1], F32, st)
                    mk = sb("mkA", [2 * HL, 1], F32, st)
                    k.red(mq, p[0:2 * HL, 0:128], ALU.max)
                    k.red(mk, p[0:2 * HL, 128:256], ALU.max)
                    k.tt("dve", mq, mq, mk, ALU.mult)
                    k.act(mq, mq, AF.Sqrt)
                    dg = sb("dgA", [2 * HL, 2 * HL], F32, st)
                    k.ts("dve", dg, ident[0:2 * HL, 0:2 * HL], mq[:, 0:1], -(32.0 ** -0.5), ALU.mult, ALU.mult)
                    p2 = ps()
                    k.mm(p2[0:1, 0:2 * HL], ones[0:2 * HL, 0:1], dg)
                    nb1 = sb("nb1A", [1, 2 * HL], F32, st)
                    k.cp("dve", nb1, p2[0:1, 0:2 * HL])
                    k.dma("sp", nbd, nb1)
                    S.barrier()

            if "B" in phases:
                with ExitStack() as st:
                    phase_B(nc, S, k, sb, ps, st, l, locals())
                    S.barrier()

            if "C" in phases:
                with ExitStack() as st:
                    phase_C(nc, S, k, sb, ps, st, l, locals())
                    S.barrier()

            if "D" in phases:
                with ExitStack() as st:
                    stg = [sb("stgD%d" % i, [128, 2048], F32, st) for i in range(2)]

                    wsb = load_w_bf16(st, "w_out_sb", w_out[l].rearrange("(kc p) n -> p kc n", p=128), MIXW // 128, D,
                                      stg)
                    tmpD = sb("tmpD", [128, D], F32, st)
                    gtmp[0] = tmpD
                    gg = [mod_tile(st, "ggD%d" % r, r, 2, "g_post_mix", False) for r in range(2)]
                    xt = [sb("xtD%d" % i, [128, D], F32, st) for i in range(2)]
                    om = [sb("omD%d" % i, [128, MIXW], BF16, st) for i in range(2)]
                    oT = [sb("oTD%d" % i, [128, MIXW // 128, 128], BF16, st) for i in range(2)]
                    of = sb("ofD", [128, D], F32, st)
                    ss = [sb("ssD%d" % i, [128, 1], F32, st) for i in range(2)]
                    sst = sb("sstD", [128, 1], F32, st)
                    for ti in range(NT):
                        if ti < NCT and not need_ctx:
                            continue
                        r = 0 if ti >= NCT else 1
                        x = xt[ti % 2]
                        o_ = om[ti % 2]
                        k.dma("sp", x, x_src(l)[ti * 128:(ti + 1) * 128, :])
                        k.dma("sp", o_, omix[ti * 128:(ti + 1) * 128, :])
                        oTt = oT[ti % 2]
                        transposes_bf16(oTt, o_, MIXW // 128)
                        for n0 in range(0, D, 512):
                            p = ps()
                            nk = MIXW // 128
                            for kc in range(nk):
                                k.mm(p, oTt[:, kc, :], wsb[:, kc, n0:n0 + 512], start=(kc == 0), stop=(kc == nk - 1))
                            k.cp("act", of[:, n0:n0 + 512], p)
                        s_ = ss[ti % 2]
                        k.act(tmpD, of, AF.Square, accum=s_)
                        rstd_of(s_, D, 1e-6, sst)
                        k.stt(tmpD, of, s_[:, 0:1], gg[r], ALU.mult, ALU.mult)
                        k.tt("dve", x, x, tmpD, ALU.add)
                        k.dma("pool", xs[ti * 128:(ti + 1) * 128, :], x)
                    S.barrier()

            if "E" in phases:
                with ExitStack() as st:
                    stg = [sb("stgE%d" % i, [128, 512], F32, st) for i in range(2)]

                    w1 = load_w_bf16(st, "w_ff1_sb", w_ff1[l].rearrange("(kc p) n -> p kc n", p=128), 8, 4 * D, stg)
                    w2 = load_w_bf16(st, "w_ff2_sb", w_ff2[l].rearrange("(kc p) n -> p kc n", p=128), 32, D, stg)
                    tmpE = sb("tmpE", [128, D], F32, st)
                    gtmp[0] = tmpE
                    gs = [mod_tile(st, "gsE%d" % r, r, 4, "g_pre_mlp", True) for r in range(2)]
                    sh = [mod_tile(st, "shE%d" % r, r, 3) for r in range(2)]
                    gg = [mod_tile(st, "ggE%d" % r, r, 5, "g_post_mlp", False) for r in range(2)]
                    xt = [sb("xtE%d" % i, [128, D], F32, st) for i in range(1)]
                    hb = [sb("hbE%d" % i, [128, D], BF16, st) for i in range(2)]
                    hT = [sb("hTE%d" % i, [128, 8, 128], BF16, st) for i in range(2)]
                    rl = [sb("rlE%d" % i, [128, 512], F32, st) for i in range(2)]
                    ab = sb("abE", [128, 4 * D], BF16, st)
                    aT = sb("aTE", [128, 32, 128], BF16, st)
                    ff = sb("ffE", [128, D], F32, st)
                    ss = [sb("ssE%d" % i, [128, 1], F32, st) for i in range(2)]
                    sst = sb("sstE", [128, 1], F32, st)
                    for ti in range(NT):
                        if ti < NCT and not need_ctx:
                            continue
                        r = 0 if ti >= NCT else 1
                        x = xt[0]
                        k.dma("sp", x, xs[ti * 128:(ti + 1) * 128, :])
                        s_ = ss[ti % 2]
                        k.act(tmpE, x, AF.Square, accum=s_)
                        rstd_of(s_, D, 1e-6, sst)
                        k.stt(tmpE, x, s_[:, 0:1], gs[r], ALU.mult, ALU.mult)
                        h = hb[ti % 2]
                        k.tt("dve", h, tmpE, sh[r], ALU.add)
                        hTt = hT[ti % 2]
                        transposes_bf16(hTt, h, 8)
                        for ci, n0 in enumerate(range(0, 4 * D, 512)):
                            p = ps()
                            for kc in range(8):
                                k.mm(p, hTt[:, kc, :], w1[:, kc, n0:n0 + 512], start=(kc == 0), stop=(kc == 7))
                            rr = rl[ci % 2]
                            k.act(rr, p, AF.Relu)
                            k.tt("pool", ab[:, n0:n0 + 512], rr, rr, ALU.mult)
                        transposes_bf16(aT, ab, 32)
                        for n0 in range(0, D, 512):
                            p = ps()
                            for kc in range(32):
                                k.mm(p, aT[:, kc, :], w2[:, kc, n0:n0 + 512], start=(kc == 0), stop=(kc == 31))
                            k.cp("act", ff[:, n0:n0 + 512], p)
                        k.act(tmpE, ff, AF.Square, accum=s_)
                        rstd_of(s_, D, 1e-6, sst)
                        k.stt(tmpE, ff, s_[:, 0:1], gg[r], ALU.mult, ALU.mult)
                        k.tt("dve", x, x, tmpE, ALU.add)
                        if l == DEPTH - 1:
                            k.dma("pool", out[(ti - NCT) * 128:(ti - NCT + 1) * 128, :], x)
                        else:
                            k.dma("pool", xs[ti * 128:(ti + 1) * 128, :], x)
                    S.barrier()

        S.barrier()
        with nc.Block() as block:
            S.emit(block)
    return nc, S


C0 = math.exp(-0.5)
import os
BSTOP = float(os.environ.get('BSTOP', '9'))


class E:
    def __init__(self, d):
        self.__dict__.update(d)


def phase_B(nc, S, k, sb, ps, st, l, env):
    e = E(env)
    HL, HW, RC, T, NT, NCT = e.HL, e.HW, e.RC, e.T, e.NT, e.NCT
    ident, ones = e.ident, e.ones
    MU, MUI, ML, MLI = e.MU, e.MUI, e.ML, e.MLI
    load_bcast, zrow, rstd_of = e.load_bcast, e.zrow, e.rstd_of
    HG = 4
    NHG = HL // HG

    def t2(name, dt=F32):
        return sb(name + "B", [128, HW], dt, st)

    muh = sb("muhB", [128, RC], F32, st)
    omu = sb("omuB", [128, RC], F32, st)
    load_bcast(muh, e.shift_mu[l:l + 1, :])
    k.ts("dve", omu, muh, -1.0, 1.0, ALU.mult, ALU.add)
    k.ts("dve", muh, muh, 0.5, None, ALU.mult)
    kkb, kab, omka, gbs, rkb, lngb, lnbb = (t2(n) for n in ("kkb", "kab", "omka", "gbs", "rkb", "lngb", "lnbb"))
    load_bcast(kkb, e.k_k[l:l + 1, :])
    load_bcast(kab, e.k_a[l:l + 1, :])
    k.ts("dve", omka, kab, -1.0, 1.0, ALU.mult, ALU.add)
    k.dma("sp", gbs, e.g_b[l])
    load_bcast(rkb, e.r_k[l:l + 1, :])
    load_bcast(lngb, e.lnx_g[l:l + 1, :])
    load_bcast(lnbb, e.lnx_b[l:l + 1, :])
    w0b, a0b, wbs, abs_ = [], [], [], []
    for d in range(2):
        w0b.append(t2("w0b%d" % d))
        a0b.append(t2("a0b%d" % d))
        load_bcast(w0b[d], e.w0[l, d:d + 1, :])
        load_bcast(a0b[d], e.a0[l, d:d + 1, :])
        wbs.append(sb("wbsB%d" % d, [64, HW], F32, st))
        abs_.append(sb("absB%d" % d, [64, HW], F32, st))
        k.dma("sp", wbs[d], e.w_b[l, d])
        k.dma("sp", abs_[d], e.a_b[l, d])

    zc = sb("zcB", [128, RC], F32, st)
    zp = sb("zpB", [128, RC], F32, st)
    zn = sb("znB", [128, RC], F32, st)
    kk0, kk, tA, tB, sig, av, kmod, bv = (t2(n) for n in ("kk0", "kk", "tA", "tB", "sig", "av", "kmod", "bv"))
    tots, e_in, e_ng, e_ex, e_rm, e_tot = (t2(n) for n in ("tots", "e_in", "e_ng", "e_ex", "e_rm", "e_tot"))
    ktl, rtl, kh, bh, khg, bhg = (t2(n) for n in ("ktl", "rtl", "kh", "bh", "khg", "bhg"))
    bg = sb("bgB", [128, 2 * HW], F32, st)
    ssk = sb("sskB", [128, HL], F32, st)
    sskt = sb("ssktB", [128, HL], F32, st)
    rkk = sb("rkkB", [128, HL], F32, st)
    txw = sb("txwB", [128, 64], F32, st)
    sxg = sb("sxgB", [128, 128], F32, st)
    txwT = sb("txwTB", [64, 128], F32, st)
    xaT = sb("xaTB", [64, 128], F32, st)
    sxgT = sb("sxgTB", [128, 128], F32, st)
    KR = sb("KRB", [64, HL, 2, 128], F32, st)
    khT = sb("khTB", [64, HL, 128], F32, st)
    bhT = sb("bhTB", [64, HL, 128], F32, st)
    AkT = sb("AkTB", [128, HL, 128], F32, st)
    BkT = sb("BkTB", [128, HL, 128], F32, st)
    BbT = sb("BbTB", [128, HL, 128], F32, st)
    Xin = sb("XinB", [128, HL, 128], F32, st)
    WXs = sb("WXsB", [128, HL, 128], F32, st)
    nX0 = sb("nX0B", [128, HL, 64], F32, st)
    ArT = sb("ArTB", [128, HL, 128], F32, st)
    Ar = sb("ArB", [128, HG, 128], F32, st)
    Pm = [sb("PmB%d" % i, [128, HG, 128], F32, st) for i in range(2)]
    Nm = [sb("NmB%d" % i, [128, HG, 128], F32, st) for i in range(2)]
    Dm = [sb("DmB%d" % i, [128, HG, 128], F32, st) for i in range(2)]
    DTm = [sb("DTmB%d" % i, [128, HG, 128], F32, st) for i in range(2)]
    HM = e.HM
    dgam = sb("dgamB", [64, HL, 64], F32, st)
    TTs = sb("TTsB", [64, HW], F32, st)
    ZTs = sb("ZTsB", [64, HL, 128], F32, st)
    Gs = sb("GsB", [64, HW], F32, st)
    Y0s = t2("Y0s")
    Hs = [sb("HsB%d" % i, [64, HW], F32, st) for i in range(2)]
    ysb = [t2("ysb%d" % i) for i in range(2)]

    def g3(v):
        return v.re("p (h d) -> p h d", d=64)

    def hs(v, h):
        return v[:, h * 64:(h + 1) * 64]

    cnt = [0]

    def ev():
        cnt[0] += 1
        return "act" if cnt[0] % 2 else "dve"

    for d in range(2):
        order = list(range(NT)) if d == 0 else (list(range(NCT - 1, -1, -1)) + list(range(NT - 1, NCT - 1, -1)))
        Ms, Mi, MsT, Tri = (MU, MUI, ML, MUI) if d == 0 else (ML, MLI, MU, MLI)
        k.memset("dve", Hs[0], 0.0)
        cur = 0
        for ti in order:
            r0 = zrow(ti)
            k.dma("sp", zc, e.zr[r0:r0 + 128, :])
            k.dma("sp", zp, e.zr[r0 - 1:r0 + 127, :])
            k.dma("sp", zn, e.zr[r0 + 1:r0 + 129, :])
            k.tt("pool", zp, zp, zn, ALU.add)
            k.tt("pool", zp, zp, muh, ALU.mult)
            k.tt("pool", zc, zc, omu, ALU.mult)
            k.tt("pool", zc, zc, zp, ALU.add)
            rr, kx, vx = zc[:, 0:HW], zc[:, HW:2 * HW], zc[:, 2 * HW:3 * HW]
            xw = zc[:, 3 * HW:3 * HW + 64]
            xa = zc[:, 3 * HW + 64:3 * HW + 128]
            xg = zc[:, 3 * HW + 128:3 * HW + 256]
            k.tt("dve", kk0, kx, kkb, ALU.mult)
            k.tt("pool", tA, kk0, kk0, ALU.mult)
            k.red(ssk, g3(tA), ALU.add)
            k.act(sskt, ssk, AF.Sqrt)
            k.ts("dve", sskt, sskt, 1e-12, None, ALU.max)
            k.recip(rkk, sskt)
            k.tt("dve", g3(kk), g3(kk0), rkk.un(2).bc([128, HL, 64]), ALU.mult)
            k.act(txw, xw, AF.Tanh)
            p = ps()
            k.tr(p[0:64, 0:128], txw, ident)
            k.tr(p[0:64, 128:256], xa, ident)
            k.cp("act", txwT, p[0:64, 0:128])
            k.cp("dve", xaT, p[0:64, 128:256])
            if d == 0:
                k.act(sxg, xg, AF.Sigmoid)
                p = ps()
                k.tr(p[:, 0:128], sxg, ident)
                k.cp("act", sxgT, p[:, 0:128])
                p = ps()
                k.mm(p[:, 0:HW], sxgT, gbs)
                k.cp("act", bg[:, HW:2 * HW], p[:, 0:HW])
                k.tt("pool", tA, rr, kx, ALU.mult)
                k.tt("pool", tA, tA, rkb, ALU.mult)
                k.red(ssk, g3(tA), ALU.add)
                k.tt("dve", g3(bg[:, 0:HW]), g3(vx), ssk.un(2).bc([128, HL, 64]), ALU.mult)
                k.dma("pool", e.bgd[ti * 128:(ti + 1) * 128, :], bg)
            p = ps()
            k.mm(p[:, 0:HW], txwT, wbs[d])
            k.tt("dve", tA, p[:, 0:HW], w0b[d], ALU.add)
            k.act(sig, tA, AF.Sigmoid)
            p = ps()
            k.mm(p[:, 0:HW], xaT, abs_[d])
            k.tt("dve", tB, p[:, 0:HW], a0b[d], ALU.add)
            k.act(av, tB, AF.Sigmoid)
            k.tt("pool", tB, av, kab, ALU.mult)
            k.tt("pool", tB, tB, omka, ALU.add)
            k.tt("pool", kmod, kx, tB, ALU.mult)
            k.tt("pool", bv, kk, av, ALU.mult)
            if BSTOP <= 1:
                continue
            pc = ps()
            k.mm(pc[:, 0:HW], Tri, sig)
            pt = ps()
            k.mm(pt[:, 0:HW], ones, sig)
            k.cp("dve", kk0, pc[:, 0:HW])
            k.cp("dve", tots, pt[:, 0:HW])
            k.act(e_in, kk0, AF.Exp, scale=-C0)
            k.act(e_ng, kk0, AF.Exp, scale=C0)
            k.tt("pool", tA, kk0, sig, ALU.subtract)
            k.act(e_ex, tA, AF.Exp, scale=-C0)
            k.tt("pool", tB, kk0, tots, ALU.subtract)
            k.act(e_rm, tB, AF.Exp, scale=C0)
            k.act(e_tot, tots, AF.Exp, scale=-C0)
            if BSTOP <= 1.2:
                continue
            k.tt("dve", ktl, kk, e_ex, ALU.mult)
            k.tt("pool", rtl, rr, e_in, ALU.mult)
            k.tt("dve", kh, kmod, e_ng, ALU.mult)
            k.tt("pool", bh, bv, e_ng, ALU.mult)
            k.tt("dve", khg, kmod, e_rm, ALU.mult)
            k.tt("pool", bhg, bv, e_rm, ALU.mult)
            if BSTOP <= 1.5:
                continue
            k.tt("dve", dgam, g3(e_tot[0:64, :]), ident[0:64, 0:64].un(1).bc([64, HL, 64]), ALU.mult)
            if BSTOP <= 1.8:
                continue
            k.cp("act", Xin[:, :, 0:64], g3(ktl))
            if BSTOP <= 2:
                continue
            for (src, dst) in ((ktl, lambda h: KR[:, h, 0, :]), (rtl, lambda h: KR[:, h, 1, :]),
                               (kh, lambda h: khT[:, h, :]), (bh, lambda h: bhT[:, h, :])):
                for h0 in range(0, HL, 4):
                    p = ps()
                    for j in range(4):
                        k.tr(p[0:64, j * 128:(j + 1) * 128], hs(src, h0 + j), ident)
                    for j in range(4):
                        k.cp(ev(), dst(h0 + j), p[0:64, j * 128:(j + 1) * 128])
            if BSTOP <= 3:
                continue
            for h0 in range(0, HL, 2):
                p = ps()
                p2 = ps()
                for j in range(2):
                    h = h0 + j
                    k.mm(p[:, j * 256:(j + 1) * 256], khT[:, h, :], KR[:, h, :, :].re("p a t -> p (a t)"))
                    k.mm(p2[:, j * 256:(j + 1) * 256], bhT[:, h, :], KR[:, h, :, :].re("p a t -> p (a t)"))
                pv = p.re("p (j a t) -> p j a t", a=2, t=128)
                p2v = p2.re("p (j a t) -> p j a t", a=2, t=128)
                m2 = lambda m: m.un(1).bc([128, 2, 128])
                k.tt("dve", AkT[:, h0:h0 + 2, :], pv[:, :, 0, :], m2(Ms), ALU.mult)
                k.tt("dve", BkT[:, h0:h0 + 2, :], pv[:, :, 1, :], m2(Mi), ALU.mult)
                k.tt("dve", BbT[:, h0:h0 + 2, :], p2v[:, :, 1, :], m2(Mi), ALU.mult)
                k.cp("act", ArT[:, h0:h0 + 2, :], p2v[:, :, 0, :])
            m4 = lambda m: m.un(1).bc([128, 4, 128])
            p = ps()
            for h in range(HL):
                k.mm(p[:, h * 64:(h + 1) * 64], AkT[:, h, :], hs(vx, h))
            k.cp("act", Xin[:, :, 64:128], p[:, 0:HW].re("p (h d) -> p h d", d=64))
            if d == 0:
                m16_ts, m16_st = HM[0], HM[1]
                mE_ts, mE_st = (HM[2], HM[4], HM[6]), (HM[3], HM[5], HM[7])
            else:
                m16_ts, m16_st = HM[1], HM[0]
                mE_ts, mE_st = (HM[3], HM[5], HM[7]), (HM[2], HM[4], HM[6])

            def mm4(lhs, rhs, h0_=0):
                p_ = ps()
                for j in range(4):
                    k.mm(p_[:, j * 128:(j + 1) * 128], lhs[:, j, :], rhs[:, j, :])
                return p_.re("p (j t) -> p j t", t=128)

            for h0 in range(0, HL, 4):
                p = ps()
                for j in range(4):
                    h = h0 + j
                    k.mm(p[:, j * 128:(j + 1) * 128], KR[:, h, 0, :], bhT[:, h, :])
                k.cp("act", Ar, p.re("p (j t) -> p j t", t=128))
                ArTg = ArT[:, h0:h0 + 4, :]
                c = 0
                k.stt(Nm[0], Ar, -1.0, m4(m16_ts), ALU.mult, ALU.mult)
                k.stt(Pm[0], ArTg, -1.0, m4(m16_st), ALU.mult, ALU.mult)
                k.tt("dve", Dm[0], Nm[0], m4(ident), ALU.add)
                k.tt("dve", DTm[0], Pm[0], m4(ident), ALU.add)
                dc = 0
                for lev in range(1, 4):
                    n_ = 1 - c
                    pp = mm4(Nm[c], Pm[c])
                    k.cp("act", Pm[n_], pp)
                    pn = mm4(Pm[c], Nm[c])
                    k.cp("dve", Nm[n_], pn)
                    pd = mm4(Pm[n_], Dm[dc])
                    k.tt("dve", Dm[1 - dc], Dm[dc], pd, ALU.add)
                    pdt = mm4(Nm[n_], DTm[dc])
                    k.tt("dve", DTm[1 - dc], DTm[dc], pdt, ALU.add)
                    c = n_
                    dc = 1 - dc
                for lev in range(3):
                    Eb, ETb, Xb, Yb = Nm[0], Pm[0], Nm[1], Pm[1]
                    k.tt("dve", Eb, Ar, m4(mE_ts[lev]), ALU.mult)
                    py = mm4(Eb, DTm[dc])
                    k.cp("act", Yb, py)
                    pdt = mm4(Dm[dc], Yb)
                    k.tt("dve", DTm[1 - dc], DTm[dc], pdt, ALU.subtract)
                    if lev < 2:
                        k.tt("dve", ETb, ArTg, m4(mE_st[lev]), ALU.mult)
                        px = mm4(ETb, Dm[dc])
                        k.cp("act", Xb, px)
                        pd = mm4(DTm[dc], Xb)
                        k.tt("dve", Dm[1 - dc], Dm[dc], pd, ALU.subtract)
                    dc = 1 - dc
                MT = DTm[dc]
                p = ps()
                for j in range(4):
                    h = h0 + j
                    k.mm(p[:, j * 128:(j + 1) * 128], MT[:, j, :], Xin[:, h, :])
                pv = p.re("p (j t) -> p j t", t=128)
                k.cp("act", WXs[:, h0:h0 + 4, :], pv)
                k.ts("dve", nX0[:, h0:h0 + 4, :], pv[:, :, 64:128], -1.0, None, ALU.mult)
            if BSTOP <= 5:
                continue
            p = ps()
            for h in range(HL):
                k.mm(p[0:64, h * 64:(h + 1) * 64], WXs[:, h, 0:64], hs(bhg, h))
            k.tt("dve", TTs, dgam.re("p h d -> p (h d)"), p[0:64, 0:HW], ALU.subtract)
            for h0 in range(0, HL, 4):
                p = ps()
                for j in range(4):
                    h = h0 + j
                    k.mm(p[0:64, j * 128:(j + 1) * 128], WXs[:, h, 0:64], BbT[:, h, :])
                k.tt("dve", ZTs[:, h0:h0 + 4, :], KR[:, h0:h0 + 4, 1, :], p[0:64, :].re("p (j t) -> p j t", t=128),
                     ALU.subtract)
            p = ps()
            for h in range(HL):
                k.mm(p[0:64, h * 64:(h + 1) * 64], hs(khg, h), hs(vx, h), start=True, stop=False)
                k.mm(p[0:64, h * 64:(h + 1) * 64], hs(bhg, h), nX0[:, h, :], start=False, stop=True)
            k.cp("act", Gs, p[0:64, 0:HW])
            p = ps()
            for h in range(HL):
                k.mm(p[:, h * 64:(h + 1) * 64], BkT[:, h, :], hs(vx, h), start=True, stop=False)
                k.mm(p[:, h * 64:(h + 1) * 64], BbT[:, h, :], nX0[:, h, :], start=False, stop=True)
            k.cp("act", Y0s, p[:, 0:HW])
            if BSTOP <= 6:
                continue
            Hc, Hn = Hs[cur], Hs[1 - cur]
            pY = ps()
            for h in range(HL):
                k.mm(pY[:, h * 64:(h + 1) * 64], ZTs[:, h, :], hs(Hc, h))
            pH = ps()
            for h in range(HL):
                k.mm(pH[0:64, h * 64:(h + 1) * 64], hs(TTs, h), hs(Hc, h))
            k.tt("dve", Hn, pH[0:64, 0:HW], Gs, ALU.add)
            y = ysb[ti % 2]
            k.tt("dve", y, pY[:, 0:HW], Y0s, ALU.add)
            k.dma("pool", e.yd[d, ti * 128:(ti + 1) * 128, :], y)
            cur = 1 - cur
    S.barrier()
    yf, yr_, cen, sq = tA, tB, kk0, kk
    ob = [e_in.bitcast(BF16)[:, 0:HW], e_ng.bitcast(BF16)[:, 0:HW]]
    mean, var, vtmp = ssk, sskt, rkk
    for ti in range(NT):
        if ti < NCT and not e.need_ctx:
            continue
        k.dma("sp", yf, e.yd[0, ti * 128:(ti + 1) * 128, :])
        k.dma("sp", yr_, e.yd[1, ti * 128:(ti + 1) * 128, :])
        k.dma("sp", bg, e.bgd[ti * 128:(ti + 1) * 128, :])
        k.tt("dve", yf, yf, yr_, ALU.add)
        k.red(mean, g3(yf), ALU.add)
        k.ts("dve", mean, mean, 1.0 / 64, None, ALU.mult)
        k.tt("dve", g3(cen), g3(yf), mean.un(2).bc([128, HL, 64]), ALU.subtract)
        k.tt("pool", sq, cen, cen, ALU.mult)
        k.red(var, g3(sq), ALU.add)
        rstd_of(var, 64, 64e-5, vtmp)
        k.tt("dve", g3(cen), g3(cen), var.un(2).bc([128, HL, 64]), ALU.mult)
        k.tt("pool", cen, cen, lngb, ALU.mult)
        k.tt("pool", cen, cen, lnbb, ALU.add)
        k.tt("pool", cen, cen, bg[:, 0:HW], ALU.add)
        o_ = ob[ti % 2]
        k.tt("dve", o_, cen, bg[:, HW:2 * HW], ALU.mult)
        k.dma("pool", e.omix[ti * 128:(ti + 1) * 128, 0:HW], o_)


def phase_C(nc, S, k, sb, ps, st, l, env):
    e = E(env)
    HL, HW, T, NT, NCT = e.HL, e.HW, e.T, e.NT, e.NCT
    psb = e.psb
    scale = 32.0 ** -0.5
    negB = sb("negBC", [128, 2 * HL], F32, st)
    e.load_bcast(negB, e.nbd[0:1, :])
    lv = {}
    for n in ("lam_q1", "lam_k1", "lam_q2", "lam_k2"):
        lv[n] = sb(n + "C", [128, 32], F32, st)
        e.load_bcast(lv[n], e.lamv[n][l:l + 1, :])
    ltmp = sb("ltmpC", [128, 32], F32, st)
    l1 = sb("l1C", [128, 1], F32, st)
    l2 = sb("l2C", [128, 1], F32, st)
    nlam = sb("nlamC", [128, 1], F32, st)
    k.tt("dve", ltmp, lv["lam_q1"], lv["lam_k1"], ALU.mult)
    k.red(l1, ltmp, ALU.add)
    k.act(l1, l1, AF.Exp)
    k.tt("dve", ltmp, lv["lam_q2"], lv["lam_k2"], ALU.mult)
    k.red(l2, ltmp, ALU.add)
    k.act(l2, l2, AF.Exp)
    k.tt("dve", nlam, l2, l1, ALU.subtract)
    k.ts("dve", nlam, nlam, -e.lam_init, None, ALU.add)
    sg = sb("sgC", [128, 64], F32, st)
    e.load_bcast(sg, e.subln_g[l:l + 1, :])
    k.ts("dve", sg, sg, 1.0 - e.lam_init, None, ALU.mult)
    vsb = sb("vsbC", [128, NT, HL * 65], BF16, st)
    k.dma("sp", vsb, e.vv.rearrange("(kt p) c -> p kt c", p=128))
    QT = [sb("QTC%d" % i, [64, T], BF16, st) for i in range(2)]
    KT = [sb("KTC%d" % i, [64, T], BF16, st) for i in range(2)]
    pT = [sb("pTC%d" % i, [128, 512], BF16, st) for i in range(3)]
    o1 = [sb("o1C%d" % i, [128, 65], F32, st) for i in range(4)]
    ot = [sb("otC%d" % i, [128, 64], F32, st) for i in range(2)]
    osq = sb("osqC", [128, 64], F32, st)
    odb = [sb("odbC%d" % i, [128, 64], BF16, st) for i in range(4)]
    r1 = sb("r1C", [128, 1], F32, st)
    r2 = sb("r2C", [128, 1], F32, st)
    ssq = sb("ssqC", [128, 1], F32, st)
    stmp = sb("stmpC", [128, 1], F32, st)
    sc_banks = psb[0:4]
    acc = psb[4:8]
    it = 0
    oi = 0
    chunks = []
    for c0 in range(NCT, NT, 4):
        chunks.append((list(range(c0, min(NT, c0 + 4))), list(range(NT))))
    if e.need_ctx:
        chunks.append((list(range(NCT)), list(range(NCT))))
    for h in range(HL):
        qt, kt_ = QT[h % 2], KT[h % 2]
        k.dma("sp", qt, e.qkT[h * 64:(h + 1) * 64, :])
        k.dma("sp", kt_, e.qkT[HW + h * 64:HW + (h + 1) * 64, :])
        for (qtiles, ktiles) in chunks:
            q0 = qtiles[0] * 128
            nq = len(qtiles) * 128
            for s in range(2):
                for ki, kt in enumerate(ktiles):
                    p = sc_banks[it % 4]
                    k.mm(p[:, 0:nq], kt_[s * 32:(s + 1) * 32, kt * 128:(kt + 1) * 128], qt[s * 32:(s + 1) * 32, q0:q0 + nq])
                    pt = pT[it % 3]
                    it += 1
                    k.act(pt[:, 0:nq], p[:, 0:nq], AF.Exp, scale=scale, bias=negB[:, 2 * h + s:2 * h + s + 1])
                    for j in range(len(qtiles)):
                        k.mm(acc[j][:, 0:65], pt[:, j * 128:(j + 1) * 128], vsb[:, kt, h * 65:(h + 1) * 65],
                             start=(ki == 0), stop=(ki == len(ktiles) - 1))
                if s == 0:
                    for j in range(len(qtiles)):
                        k.cp("act", o1[j], acc[j][:, 0:65])
                else:
                    for j, qti in enumerate(qtiles):
                        o_ = ot[oi % 2]
                        ob_ = odb[oi % 4]
                        oi += 1
                        k.recip(r1, o1[j][:, 64:65])
                        k.recip(r2, acc[j][:, 64:65])
                        k.tt("dve", r2, r2, nlam, ALU.mult)
                        k.ts("dve", o_, o1[j][:, 0:64], r1[:, 0:1], None, ALU.mult)
                        k.stt(o_, acc[j][:, 0:64], r2[:, 0:1], o_, ALU.mult, ALU.add)
                        k.tt("dve", osq, o_, o_, ALU.mult)
                        k.red(ssq, osq, ALU.add)
                        e.rstd_of(ssq, 64, 1e-5, stmp)
                        k.stt(ob_, o_, ssq[:, 0:1], sg, ALU.mult, ALU.mult)
                        k.dma("pool", e.omix[qti * 128:(qti + 1) * 128, HW + h * 64:HW + (h + 1) * 64], ob_)


def _consts(L):
    T = CTX + L
    ident = np.eye(128, dtype=np.float32)
    i = np.arange(128)
    U = i[:, None] < i[None, :]
    UI = i[:, None] <= i[None, :]
    LO = i[:, None] > i[None, :]
    LI = i[:, None] >= i[None, :]
    masks = np.concatenate([U, UI, LO, LI], axis=1).astype(np.float32)
    ii, jj = i[:, None], i[None, :]
    L16 = (ii // 16 == jj // 16) & (jj < ii)
    hm = [L16, L16.T]
    for b in (16, 32, 64):
        EL = (ii // (2 * b) == jj // (2 * b)) & (ii // b == jj // b + 1)
        hm += [EL, EL.T]
    masks2 = np.concatenate(hm, axis=1).astype(np.float32)
    t = np.arange(L)
    inv = 10000.0 ** (-np.arange(8, dtype=np.float64) / 8.0)
    ar = (t // GRID_W)[:, None] * inv[None, :]
    ac = (t % GRID_W)[:, None] * inv[None, :]
    cos32 = np.concatenate([np.cos(ar), np.cos(ar), np.cos(ac), np.cos(ac)], axis=1)
    sin32 = np.concatenate([-np.sin(ar), np.sin(ar), -np.sin(ac), np.sin(ac)], axis=1)
    rope = np.zeros((T, 64), np.float32)
    rope[:CTX, 0:32] = 1.0
    rope[CTX:, 0:32] = cos32
    rope[CTX:, 32:64] = sin32
    return ident, masks, masks2, rope


_CACHE = {}


def run(cfg, inputs, ncores=8):
    L = cfg["L"]
    DEPTH = cfg["DEPTH"]
    key = (L, DEPTH, tuple(cfg.get("dbg", ())), cfg.get("phases", "MABCDE"))
    if key not in _CACHE:
        _CACHE[key] = build(cfg)
    nc, S = _CACHE[key]
    f = lambda a: np.ascontiguousarray(np.asarray(a, dtype=np.float32))
    ident, masks, masks2, rope = _consts(L)
    B = inputs["x"].shape[0]
    shared = {n: f(inputs[n]) for n in (
        "ada_w", "ada_b", "g_pre_mix", "g_post_mix", "g_pre_mlp", "g_post_mlp", "w_in", "shift_mu", "k_k", "k_a",
        "w0", "w_b", "a0", "a_b", "g_b", "lnx_g", "lnx_b", "lam_q1", "lam_k1", "lam_q2", "lam_k2", "subln_g",
        "w_out", "w_ff1", "w_ff2")}
    shared["r_k"] = f(inputs["r_k"]).reshape(DEPTH, 512)
    shared["c_ident"] = ident
    shared["c_masks"] = masks
    shared["c_masks2"] = masks2
    shared["c_rope"] = rope
    in_maps = []
    for c in range(ncores):
        b = c % B
        m = dict(shared)
        m["xin"] = np.ascontiguousarray(np.concatenate([f(inputs["ctx"][b]), f(inputs["x"][b])], axis=0))
        m["c2"] = np.ascontiguousarray(np.stack([f(inputs["c"][b]), f(inputs["c_ctx"])], axis=0))
        in_maps.append(m)
    res = run_bass_kernel_spmd(nc, in_maps, core_ids=list(range(ncores)))
    return res.results


def kernel(**inputs):
    cfg = dict(L=8192, DEPTH=4)
    results = run(cfg, inputs, 8)
    B = inputs["x"].shape[0]
    return np.stack([np.asarray(results[b]["out"], dtype=np.float32) for b in range(B)], axis=0)
```

```python
import math
from contextlib import ExitStack
import numpy as np
import ml_dtypes
import concourse.bass as bass
import concourse.mybir as mybir
from concourse.bass_utils import run_bass_kernel_spmd

F32 = mybir.dt.float32
BF16 = mybir.dt.bfloat16
ALU = mybir.AluOpType
AF = mybir.ActivationFunctionType
AX = mybir.AxisListType

D = 1024
CTX = 256
GRID_W = 64
NDS = 40
ENG = ["pe", "dve", "act", "pool", "sp"]


class Buf:
    __slots__ = ("lw", "rd", "name", "excl")

    def __init__(self, name="", excl=False):
        self.lw = None
        self.rd = {}
        self.name = name
        self.excl = excl


class V:
    def __init__(self, ap, buf):
        self.ap = ap
        self.buf = buf

    def __getitem__(self, k):
        return V(self.ap[k], self.buf)

    def re(self, s, **kw):
        return V(self.ap.rearrange(s, **kw), self.buf)

    def bc(self, shape):
        return V(self.ap.broadcast_to(shape), self.buf)

    def un(self, ax):
        return V(self.ap.unsqueeze(ax), self.buf)

    def bitcast(self, dt):
        return V(self.ap.bitcast(dt), self.buf)


class Sched:
    def __init__(self, nc, es):
        self.nc = nc
        self.es = es
        self.prog = {e: [] for e in ENG}
        self.sem = {e: es.enter_context(nc.semaphore("c_" + e)) for e in ENG}
        self.cnt = {e: 0 for e in ENG}
        self.seen = {e: {} for e in ENG}
        self.dsem = [es.enter_context(nc.semaphore("d%d" % i)) for i in range(NDS)]
        self.dval = [0] * NDS
        self.dnext = 0
        self.ninst = 0

    def _wait(self, eng, key, n):
        if key[0] == "e" and key[1] == eng and eng == "pe":
            return
        if self.seen[eng].get(key, 0) >= n:
            return
        self.seen[eng][key] = n
        sem = self.sem[key[1]] if key[0] == "e" else self.dsem[key[1]]
        self.prog[eng].append(("w", sem, n))

    def _deps(self, eng, reads, writes):
        for b in reads:
            if b.lw is not None:
                self._wait(eng, b.lw[0], b.lw[1])
            if b.excl:
                for k, n in b.rd.items():
                    if k != ("e", eng):
                        self._wait(eng, k, n)
        for b in writes:
            if b.lw is not None:
                self._wait(eng, b.lw[0], b.lw[1])
            for k, n in b.rd.items():
                self._wait(eng, k, n)

    def _mark(self, t, reads, writes):
        for b in reads:
            if b.rd.get(t[0], 0) < t[1]:
                b.rd[t[0]] = t[1]
        for b in writes:
            b.lw = t
            b.rd = {}

    def op(self, eng, fn, reads=(), writes=()):
        self._deps(eng, reads, writes)
        self.cnt[eng] += 1
        t = (("e", eng), self.cnt[eng])
        self.prog[eng].append(("i", fn))
        self._mark(t, reads, writes)
        self.ninst += 1
        return t

    def dma(self, q, out, in_, reads=(), writes=()):
        idx = self.dnext
        self.dnext = (self.dnext + 1) % NDS
        if self.dval[idx] > 0:
            self._wait(q, ("d", idx), self.dval[idx])
        self._deps(q, reads, writes)
        self.dval[idx] += 16
        t = (("d", idx), self.dval[idx])
        self.prog[q].append(("d", out, in_, self.dsem[idx]))
        self._mark(t, reads, writes)
        self.ninst += 1
        return t

    def barrier(self):
        for e in ENG:
            for e2 in ENG:
                if e2 != e and self.cnt[e2] > 0:
                    self._wait(e, ("e", e2), self.cnt[e2])
            for i in range(NDS):
                if self.dval[i] > 0:
                    self._wait(e, ("d", i), self.dval[i])

    def emit(self, block):
        names = dict(pe="tensor", dve="vector", act="scalar", pool="gpsimd", sp="sync")
        for e in ENG:
            prog = self.prog[e]
            sem = self.sem[e]

            def body(h, prog=prog, sem=sem):
                for it in prog:
                    if it[0] == "w":
                        h.wait_ge(it[1], it[2])
                    elif it[0] == "i":
                        it[1](h).then_inc(sem, 1)
                    else:
                        h.dma_start(out=it[1], in_=it[2]).then_inc(it[3], 16)

            getattr(block, names[e])(body)


def _bufs(*vs):
    return [v.buf for v in vs if isinstance(v, V)]


def _a(x):
    return x.ap if isinstance(x, V) else x


class K:
    def __init__(self, S):
        self.S = S

    def tt(self, eng, out, a, b, op):
        return self.S.op(eng, lambda h, o=out.ap, x=a.ap, y=b.ap: h.tensor_tensor(out=o, in0=x, in1=y, op=op),
                         _bufs(a, b), _bufs(out))

    def ts(self, eng, out, a, s1, s2, op0, op1=None, accum=None):
        kw = {}
        if op1 is not None:
            kw["op1"] = op1
        if accum is not None:
            kw["accum_out"] = accum.ap
        return self.S.op(eng, lambda h, o=out.ap, x=a.ap, p=_a(s1), q=_a(s2): h.tensor_scalar(
            out=o, in0=x, scalar1=p, scalar2=q, op0=op0, **kw), _bufs(a, s1, s2), _bufs(out, accum))

    def stt(self, out, a, s, b, op0, op1):
        return self.S.op("dve", lambda h, o=out.ap, x=a.ap, p=_a(s), y=b.ap: h.scalar_tensor_tensor(
            out=o, in0=x, scalar=p, in1=y, op0=op0, op1=op1), _bufs(a, s, b), _bufs(out))

    def act(self, out, a, func, scale=1.0, bias=None, accum=None):
        kw = {}
        if bias is not None:
            kw["bias"] = _a(bias)
        if accum is not None:
            kw["accum_out"] = accum.ap
        return self.S.op("act", lambda h, o=out.ap, x=a.ap, sc=_a(scale): h.activation(
            out=o, in_=x, func=func, scale=sc, **kw), _bufs(a, scale, bias), _bufs(out, accum))

    def cp(self, eng, out, a):
        if eng == "act":
            return self.act(out, a, AF.Copy)
        return self.S.op(eng, lambda h, o=out.ap, x=a.ap: h.tensor_copy(out=o, in_=x), _bufs(a), _bufs(out))

    def red(self, out, a, op, axis=AX.X):
        return self.S.op("dve", lambda h, o=out.ap, x=a.ap: h.tensor_reduce(out=o, in_=x, axis=axis, op=op),
                         _bufs(a), _bufs(out))

    def recip(self, out, a):
        return self.S.op("dve", lambda h, o=out.ap, x=a.ap: h.reciprocal(out=o, in_=x), _bufs(a), _bufs(out))

    def memset(self, eng, out, val):
        return self.S.op(eng, lambda h, o=out.ap: h.memset(o, val), [], _bufs(out))

    def mm(self, out, lhsT, rhs, start=True, stop=True):
        return self.S.op("pe", lambda h, o=out.ap, l=lhsT.ap, r=rhs.ap: h.matmul(o, l, r, start=start, stop=stop),
                         _bufs(lhsT, rhs), _bufs(out))

    def tr(self, out, a, ident):
        return self.S.op("pe", lambda h, o=out.ap, x=a.ap, i=ident.ap: h.transpose(o, x, i),
                         _bufs(a, ident), _bufs(out))

    def dma(self, q, out, in_):
        return self.S.dma(q, _a(out), _a(in_), _bufs(in_), _bufs(out))


def build(cfg):
    L = cfg["L"]
    DEPTH = cfg["DEPTH"]
    HL = cfg.get("HL", 8)
    dbg = cfg.get("dbg", ())
    phases = cfg.get("phases", "MABCDE")
    T = CTX + L
    NT = T // 128
    NCT = CTX // 128
    HW = 64 * HL
    RC = 3 * HW + 256
    ZC = RC + 3 * HW
    MIXW = 2 * HW

    nc = bass.Bass("TRN2", target_bir_lowering=False)

    def din(name, shape, dt=F32):
        return nc.dram_tensor(name, list(shape), dt, kind="ExternalInput").ap()

    def dint(name, shape, dt=F32):
        kind = "ExternalOutput" if name in dbg else "Internal"
        return nc.dram_tensor(name, list(shape), dt, kind=kind).ap()

    xin = din("xin", [T, D])
    c2 = din("c2", [2, D])
    ada_w = din("ada_w", [DEPTH, D, 6 * D])
    ada_b = din("ada_b", [DEPTH, 6 * D])
    gvec = {n: din(n, [DEPTH, D]) for n in ("g_pre_mix", "g_post_mix", "g_pre_mlp", "g_post_mlp")}
    w_in = din("w_in", [DEPTH, D, ZC])
    shift_mu = din("shift_mu", [DEPTH, RC])
    k_k = din("k_k", [DEPTH, HW])
    k_a = din("k_a", [DEPTH, HW])
    w0 = din("w0", [DEPTH, 2, HW])
    w_b = din("w_b", [DEPTH, 2, 64, HW])
    a0 = din("a0", [DEPTH, 2, HW])
    a_b = din("a_b", [DEPTH, 2, 64, HW])
    g_b = din("g_b", [DEPTH, 128, HW])
    r_k = din("r_k", [DEPTH, HW])
    lnx_g = din("lnx_g", [DEPTH, HW])
    lnx_b = din("lnx_b", [DEPTH, HW])
    lamv = {n: din(n, [DEPTH, 32]) for n in ("lam_q1", "lam_k1", "lam_q2", "lam_k2")}
    subln_g = din("subln_g", [DEPTH, 64])
    w_out = din("w_out", [DEPTH, MIXW, D])
    w_ff1 = din("w_ff1", [DEPTH, D, 4 * D])
    w_ff2 = din("w_ff2", [DEPTH, 4 * D, D])
    c_ident = din("c_ident", [128, 128])
    c_masks = din("c_masks", [128, 512])
    c_masks2 = din("c_masks2", [128, 1024])
    c_rope = din("c_rope", [T, 64])

    out = nc.dram_tensor("out", [L, D], F32, kind="ExternalOutput").ap()
    xs = dint("xs", [T, D])
    modd = dint("modd", [2, 6 * D])
    zr = dint("zr", [T + 3, RC])
    qkT = dint("qkT", [2 * HW, T], BF16)
    vv = dint("vv", [T, HL * 65], BF16)
    yd = dint("yd", [2, T, HW])
    bgd = dint("bgd", [T, 2 * HW])
    omix = dint("omix", [T, MIXW], BF16)
    nbd = dint("nbd", [1, 2 * HL])

    def zrow(ti):
        return 1 + ti * 128 if ti < NCT else CTX + 2 + (ti - NCT) * 128

    es = ExitStack()
    with es:
        S = Sched(nc, es)
        k = K(S)

        uid = [0]

        def sb(name, shape, dt=F32, stack=None):
            uid[0] += 1
            name = "%s_%d" % (name, uid[0])
            t = (stack or es).enter_context(nc.sbuf_tensor(name, list(shape), dt))
            return V(t[:] if len(shape) == 2 else t[:], Buf(name))

        psb = []
        for i in range(8):
            t = es.enter_context(nc.psum_tensor("ps%d" % i, [128, 512], F32))
            psb.append(V(t[:], Buf("ps%d" % i, True)))
        psn = [0]

        def ps():
            v = psb[psn[0] % 8]
            psn[0] += 1
            return v

        ident = sb("ident", [128, 128])
        identb = sb("identb", [128, 128], BF16)
        ones = sb("ones", [128, 128])
        sc = sb("sc", [128, 8, 2])
        sctmp = sb("sctmp", [128, 2, 8])
        st0 = ExitStack()
        zrow_sb = sb("zrow_sb", [1, RC], F32, st0)
        k.dma("sp", ident, c_ident)
        k.cp("dve", identb, ident)
        k.memset("dve", ones, 1.0)
        k.memset("dve", zrow_sb, 0.0)
        for r in range(2):
            k.dma("sp", sctmp[:, r, :], c2[r].rearrange("(p kc) -> p kc", kc=8))
        k.act(sc.re("p kc r -> p r kc"), sctmp, AF.Silu)
        for row in (0, CTX + 1, T + 2):
            k.dma("sp", zr[row:row + 1, :], zrow_sb)
        S.barrier()
        st0.close()
        HM = [sb("hmask%d" % i, [128, 128]) for i in range(8)]
        for i, m_ in enumerate(HM):
            k.dma("sp", m_, c_masks2[:, i * 128:(i + 1) * 128])
        MU, MUI, ML, MLI = (sb("mask%d" % i, [128, 128]) for i in range(4))
        for i, m_ in enumerate((MU, MUI, ML, MLI)):
            k.dma("sp", m_, c_masks[:, i * 128:(i + 1) * 128])
        S.barrier()

        def rstd_of(ss, n, eps, tmp):
            k.ts("dve", tmp, ss, 1.0 / n, eps, ALU.mult, ALU.add)
            k.act(tmp, tmp, AF.Sqrt)
            k.recip(ss, tmp)

        def load_bcast(dst, src_row):
            k.dma("sp", dst, src_row.broadcast_to([128, src_row.shape[-1]]))

        def load_w_bf16(stack, name, src3, nk, ncols, stg):
            w = sb(name, [128, nk, ncols], BF16, stack)
            engs = ["pool", "act", "dve"]
            i = 0
            for kc in range(nk):
                sw = stg[0].ap.shape[1]
                for n0 in range(0, ncols, sw):
                    n1 = min(ncols, n0 + sw)
                    s = stg[i % len(stg)]
                    k.dma("sp", s[:, 0:n1 - n0], src3[:, kc, n0:n1])
                    k.cp(engs[i % 3], w[:, kc, n0:n1], s[:, 0:n1 - n0])
                    i += 1
            return w

        def transposes_bf16(dstT, src, nchunks):
            for c0 in range(0, nchunks, 8):
                cn = min(8, nchunks - c0)
                p = ps()
                pb = p.bitcast(BF16)
                for j in range(cn):
                    k.tr(pb[:, j * 128:(j + 1) * 128], src[:, (c0 + j) * 128:(c0 + j + 1) * 128], identb)
                k.cp("act", dstT[:, c0:c0 + cn, :], pb[:, 0:cn * 128].re("p (c t) -> p c t", t=128))

        def x_src(l):
            return xin if l == 0 else xs

        for l in range(DEPTH):
            need_ctx = l < DEPTH - 1
            lam_init = 0.8 - 0.6 * math.exp(-0.3 * l)
            if "M" in phases:
                with ExitStack() as st:
                    adab = sb("adab", [2, 6 * D], F32, st)
                    modsb = sb("modsb", [2, 6 * D], F32, st)
                    wch = [sb("wch%d" % i, [128, 8, 512], F32, st) for i in range(2)]
                    k.dma("sp", adab, ada_b[l:l + 1, :].broadcast_to([2, 6 * D]))
                    aw = ada_w[l].rearrange("(p kc) n -> p kc n", kc=8)
                    for ci in range(12):
                        w = wch[ci % 2]
                        k.dma("sp", w, aw[:, :, ci * 512:(ci + 1) * 512])
                        p = ps()
                        for kc in range(8):
                            k.mm(p[0:2, :], sc[:, kc, :], w[:, kc, :], start=(kc == 0), stop=(kc == 7))
                        k.tt("dve", modsb[:, ci * 512:(ci + 1) * 512], p[0:2, :], adab[:, ci * 512:(ci + 1) * 512],
                             ALU.add)
                    k.dma("sp", modd, modsb)
                    S.barrier()

            gtmp = [None]

            def mod_tile(stack, name, row, seg, gname=None, plus1=False):
                t = sb(name, [128, D], F32, stack)
                load_bcast(t, modd[row:row + 1, seg * D:(seg + 1) * D])
                if gname is not None:
                    g = gtmp[0]
                    load_bcast(g, gvec[gname][l:l + 1, :])
                    if plus1:
                        k.stt(t, t, 1.0, g, ALU.add, ALU.mult)
                    else:
                        k.tt("dve", t, t, g, ALU.mult)
                return t

            if "A" in phases:
                with ExitStack() as st:
                    stg = [sb("stgA%d" % i, [128, 2048], F32, st) for i in range(2)]

                    wsb = load_w_bf16(st, "w_in_sb", w_in[l].rearrange("(kc p) n -> p kc n", p=128), 8, ZC, stg)
                    tmpA = sb("tmpA", [128, D], F32, st)
                    gtmp[0] = tmpA
                    gs = [mod_tile(st, "gsA%d" % r, r, 1, "g_pre_mix", True) for r in range(2)]
                    sh = [mod_tile(st, "shA%d" % r, r, 0) for r in range(2)]
                    xt = [sb("xtA%d" % i, [128, D], F32, st) for i in range(2)]
                    hb = [sb("hbA%d" % i, [128, D], BF16, st) for i in range(2)]
                    hT = [sb("hTA%d" % i, [128, 8, 128], BF16, st) for i in range(2)]
                    zst = [sb("zstA%d" % i, [128, RC], F32, st) for i in range(2)]
                    qkf = sb("qkfA", [128, 2 * HW], F32, st)
                    qk1 = sb("qk1A", [128, 2 * HW], F32, st)
                    qk2 = sb("qk2A", [128, 2 * HW], F32, st)
                    qkb = sb("qkbA", [128, 2 * HW], BF16, st)
                    qkTs = [sb("qkTsA%d" % i, [128, 2 * HW // 128, 128], BF16, st) for i in range(2)]
                    vst = [sb("vstA%d" % i, [128, HL, 65], BF16, st) for i in range(2)]
                    rp = [sb("rpA%d" % i, [128, 64], F32, st) for i in range(2)]
                    ss = [sb("ssA%d" % i, [128, 1], F32, st) for i in range(2)]
                    sst = sb("sstA", [128, 1], F32, st)
                    nmax = sb("nmaxA", [128, 2 * HL * 2], F32, st)
                    nsq = sb("nsqA", [128, 2 * HL * 2], F32, st)
                    k.memset("dve", nmax, 0.0)
                    for i in range(2):
                        k.memset("pool", vst[i][:, :, 64:65], 1.0)
                    NG = 2 * HW // 32
                    for ti in range(NT):
                        r = 0 if ti >= NCT else 1
                        x = xt[ti % 2]
                        k.dma("sp", x, x_src(l)[ti * 128:(ti + 1) * 128, :])
                        k.dma("sp", rp[ti % 2], c_rope[ti * 128:(ti + 1) * 128, :])
                        s_ = ss[ti % 2]
                        k.act(tmpA, x, AF.Square, accum=s_)
                        rstd_of(s_, D, 1e-6, sst)
                        k.stt(tmpA, x, s_[:, 0:1], gs[r], ALU.mult, ALU.mult)
                        h = hb[ti % 2]
                        k.tt("dve", h, tmpA, sh[r], ALU.add)
                        hTt = hT[ti % 2]
                        transposes_bf16(hTt, h, 8)
                        z = zst[ti % 2]
                        vs_ = vst[ti % 2]
                        for n0 in range(0, ZC, 512):
                            n1 = min(ZC, n0 + 512)
                            p = ps()
                            for kc in range(8):
                                k.mm(p[:, 0:n1 - n0], hTt[:, kc, :], wsb[:, kc, n0:n1], start=(kc == 0), stop=(kc == 7))
                            a0_, a1_ = n0, min(n1, RC)
                            if a1_ > a0_:
                                k.cp("act", z[:, a0_:a1_], p[:, a0_ - n0:a1_ - n0])
                            b0_, b1_ = max(n0, RC), min(n1, RC + 2 * HW)
                            if b1_ > b0_:
                                k.cp("act", qkf[:, b0_ - RC:b1_ - RC], p[:, b0_ - n0:b1_ - n0])
                            c0_, c1_ = max(n0, RC + 2 * HW), n1
                            if c1_ > c0_:
                                hh0 = (c0_ - RC - 2 * HW) // 64
                                hh1 = (c1_ - RC - 2 * HW) // 64
                                k.cp("act", vs_[:, hh0:hh1, 0:64],
                                     p[:, c0_ - n0:c1_ - n0].re("p (h d) -> p h d", d=64))
                        k.dma("pool", zr[zrow(ti):zrow(ti) + 128, :], z)
                        k.dma("pool", vv[ti * 128:(ti + 1) * 128, :], vs_.re("p h d -> p (h d)"))
                        rpt = rp[ti % 2]
                        cosb = rpt[:, 0:32].un(1).bc([128, NG, 32])
                        k.tt("dve", qk1.re("p (g d) -> p g d", d=32), qkf.re("p (g d) -> p g d", d=32), cosb, ALU.mult)
                        x4 = qkf.re("p (g a e) -> p g a e", a=2, e=8)
                        o4 = qk2.re("p (g a e) -> p g a e", a=2, e=8)
                        s4 = rpt[:, 32:64].re("p (c a e) -> p c a e", a=2, e=8)
                        for aa in range(2):
                            for cc in range(2):
                                xin_ = x4[:, cc::2, 1 - aa, :]
                                oo_ = o4[:, cc::2, aa, :]
                                sn_ = s4[:, cc, aa, :].un(1).bc([128, NG, 8])
                                k.tt("pool", oo_, xin_, sn_, ALU.mult)
                        k.tt("dve", qk1, qk1, qk2, ALU.add)
                        k.cp("act", qkb, qk1)
                        k.tt("pool", qk2, qk1, qk1, ALU.mult)
                        k.red(nsq, qk2.re("p (g d) -> p g d", d=32), ALU.add)
                        k.tt("dve", nmax, nmax, nsq, ALU.max)
                        qT = qkTs[ti % 2]
                        transposes_bf16(qT, qkb, 2 * HW // 128)
                        k.dma("pool", qkT[:, ti * 128:(ti + 1) * 128].rearrange("(c p) t -> p c t", p=128), qT)
                    p = ps()
                    k.tr(p[0:2 * HL, 0:128], nmax[:, 0:2 * HL], ident)
                    k.tr(p[0:2 * HL, 128:256], nmax[:, 2 * HL:4 * HL], ident)
                    mq = sb("mqA", [2 * HL, 1], F32, st)
                    mk = sb("mkA", [2 * HL, 1], F32, st)
                    k.red(mq, p[0:2 * HL, 0:128], ALU.max)
                    k.red(mk, p[0:2 * HL, 128:256], ALU.max)
                    k.tt("dve", mq, mq, mk, ALU.mult)
                    k.act(mq, mq, AF.Sqrt)
                    dg = sb("dgA", [2 * HL, 2 * HL], F32, st)
                    k.ts("dve", dg, ident[0:2 * HL, 0:2 * HL], mq[:, 0:1], -(32.0 ** -0.5), ALU.mult, ALU.mult)
                    p2 = ps()
                    k.mm(p2[0:1, 0:2 * HL], ones[0:2 * HL, 0:1], dg)
                    nb1 = sb("nb1A", [1, 2 * HL], F32, st)
                    k.cp("dve", nb1, p2[0:1, 0:2 * HL])
                    k.dma("sp", nbd, nb1)
                    S.barrier()

            if "B" in phases:
                with ExitStack() as st:
                    phase_B(nc, S, k, sb, ps, st, l, locals())
                    S.barrier()

            if "C" in phases:
                with ExitStack() as st:
                    phase_C(nc, S, k, sb, ps, st, l, locals())
                    S.barrier()

            if "D" in phases:
                with ExitStack() as st:
                    stg = [sb("stgD%d" % i, [128, 2048], F32, st) for i in range(2)]

                    wsb = load_w_bf16(st, "w_out_sb", w_out[l].rearrange("(kc p) n -> p kc n", p=128), MIXW // 128, D,
                                      stg)
                    tmpD = sb("tmpD", [128, D], F32, st)
                    gtmp[0] = tmpD
                    gg = [mod_tile(st, "ggD%d" % r, r, 2, "g_post_mix", False) for r in range(2)]
                    xt = [sb("xtD%d" % i, [128, D], F32, st) for i in range(2)]
                    om = [sb("omD%d" % i, [128, MIXW], BF16, st) for i in range(2)]
                    oT = [sb("oTD%d" % i, [128, MIXW // 128, 128], BF16, st) for i in range(2)]
                    of = sb("ofD", [128, D], F32, st)
                    ss = [sb("ssD%d" % i, [128, 1], F32, st) for i in range(2)]
                    sst = sb("sstD", [128, 1], F32, st)
                    for ti in range(NT):
                        if ti < NCT and not need_ctx:
                            continue
                        r = 0 if ti >= NCT else 1
                        x = xt[ti % 2]
                        o_ = om[ti % 2]
                        k.dma("sp", x, x_src(l)[ti * 128:(ti + 1) * 128, :])
                        k.dma("sp", o_, omix[ti * 128:(ti + 1) * 128, :])
                        oTt = oT[ti % 2]
                        transposes_bf16(oTt, o_, MIXW // 128)
                        for n0 in range(0, D, 512):
                            p = ps()
                            nk = MIXW // 128
                            for kc in range(nk):
                                k.mm(p, oTt[:, kc, :], wsb[:, kc, n0:n0 + 512], start=(kc == 0), stop=(kc == nk - 1))
                            k.cp("act", of[:, n0:n0 + 512], p)
                        s_ = ss[ti % 2]
                        k.act(tmpD, of, AF.Square, accum=s_)
                        rstd_of(s_, D, 1e-6, sst)
                        k.stt(tmpD, of, s_[:, 0:1], gg[r], ALU.mult, ALU.mult)
                        k.tt("dve", x, x, tmpD, ALU.add)
                        k.dma("pool", xs[ti * 128:(ti + 1) * 128, :], x)
                    S.barrier()

            if "E" in phases:
                with ExitStack() as st:
                    stg = [sb("stgE%d" % i, [128, 512], F32, st) for i in range(2)]

                    w1 = load_w_bf16(st, "w_ff1_sb", w_ff1[l].rearrange("(kc p) n -> p kc n", p=128), 8, 4 * D, stg)
                    w2 = load_w_bf16(st, "w_ff2_sb", w_ff2[l].rearrange("(kc p) n -> p kc n", p=128), 32, D, stg)
                    tmpE = sb("tmpE", [128, D], F32, st)
                    gtmp[0] = tmpE
                    gs = [mod_tile(st, "gsE%d" % r, r, 4, "g_pre_mlp", True) for r in range(2)]
                    sh = [mod_tile(st, "shE%d" % r, r, 3) for r in range(2)]
                    gg = [mod_tile(st, "ggE%d" % r, r, 5, "g_post_mlp", False) for r in range(2)]
                    xt = [sb("xtE%d" % i, [128, D], F32, st) for i in range(1)]
                    hb = [sb("hbE%d" % i, [128, D], BF16, st) for i in range(2)]
                    hT = [sb("hTE%d" % i, [128, 8, 128], BF16, st) for i in range(2)]
                    rl = [sb("rlE%d" % i, [128, 512], F32, st) for i in range(2)]
                    ab = sb("abE", [128, 4 * D], BF16, st)
                    aT = sb("aTE", [128, 32, 128], BF16, st)
                    ff = sb("ffE", [128, D], F32, st)
                    ss = [sb("ssE%d" % i, [128, 1], F32, st) for i in range(2)]
                    sst = sb("sstE", [128, 1], F32, st)
                    for ti in range(NT):
                        if ti < NCT and not need_ctx:
                            continue
                        r = 0 if ti >= NCT else 1
                        x = xt[0]
                        k.dma("sp", x, xs[ti * 128:(ti + 1) * 128, :])
                        s_ = ss[ti % 2]
                        k.act(tmpE, x, AF.Square, accum=s_)
                        rstd_of(s_, D, 1e-6, sst)
                        k.stt(tmpE, x, s_[:, 0:1], gs[r], ALU.mult, ALU.mult)
                        h = hb[ti % 2]
                        k.tt("dve", h, tmpE, sh[r], ALU.add)
                        hTt = hT[ti % 2]
                        transposes_bf16(hTt, h, 8)
                        for ci, n0 in enumerate(range(0, 4 * D, 512)):
                            p = ps()
                            for kc in range(8):
                                k.mm(p, hTt[:, kc, :], w1[:, kc, n0:n0 + 512], start=(kc == 0), stop=(kc == 7))
                            rr = rl[ci % 2]
                            k.act(rr, p, AF.Relu)
                            k.tt("pool", ab[:, n0:n0 + 512], rr, rr, ALU.mult)
                        transposes_bf16(aT, ab, 32)
                        for n0 in range(0, D, 512):
                            p = ps()
                            for kc in range(32):
                                k.mm(p, aT[:, kc, :], w2[:, kc, n0:n0 + 512], start=(kc == 0), stop=(kc == 31))
                            k.cp("act", ff[:, n0:n0 + 512], p)
                        k.act(tmpE, ff, AF.Square, accum=s_)
                        rstd_of(s_, D, 1e-6, sst)
                        k.stt(tmpE, ff, s_[:, 0:1], gg[r], ALU.mult, ALU.mult)
                        k.tt("dve", x, x, tmpE, ALU.add)
                        if l == DEPTH - 1:
                            k.dma("pool", out[(ti - NCT) * 128:(ti - NCT + 1) * 128, :], x)
                        else:
                            k.dma("pool", xs[ti * 128:(ti + 1) * 128, :], x)
                    S.barrier()

        S.barrier()
        with nc.Block() as block:
            S.emit(block)
    return nc, S


C0 = math.exp(-0.5)
import os
BSTOP = float(os.environ.get('BSTOP', '9'))


class E:
    def __init__(self, d):
        self.__dict__.update(d)


def phase_B(nc, S, k, sb, ps, st, l, env):
    e = E(env)
    HL, HW, RC, T, NT, NCT = e.HL, e.HW, e.RC, e.T, e.NT, e.NCT
    ident, ones = e.ident, e.ones
    MU, MUI, ML, MLI = e.MU, e.MUI, e.ML, e.MLI
    load_bcast, zrow, rstd_of = e.load_bcast, e.zrow, e.rstd_of
    HG = 4
    NHG = HL // HG

    def t2(name, dt=F32):
        return sb(name + "B", [128, HW], dt, st)

    muh = sb("muhB", [128, RC], F32, st)
    omu = sb("omuB", [128, RC], F32, st)
    load_bcast(muh, e.shift_mu[l:l + 1, :])
    k.ts("dve", omu, muh, -1.0, 1.0, ALU.mult, ALU.add)
    k.ts("dve", muh, muh, 0.5, None, ALU.mult)
    kkb, kab, omka, gbs, rkb, lngb, lnbb = (t2(n) for n in ("kkb", "kab", "omka", "gbs", "rkb", "lngb", "lnbb"))
    load_bcast(kkb, e.k_k[l:l + 1, :])
    load_bcast(kab, e.k_a[l:l + 1, :])
    k.ts("dve", omka, kab, -1.0, 1.0, ALU.mult, ALU.add)
    k.dma("sp", gbs, e.g_b[l])
    load_bcast(rkb, e.r_k[l:l + 1, :])
    load_bcast(lngb, e.lnx_g[l:l + 1, :])
    load_bcast(lnbb, e.lnx_b[l:l + 1, :])
    w0b, a0b, wbs, abs_ = [], [], [], []
    for d in range(2):
        w0b.append(t2("w0b%d" % d))
        a0b.append(t2("a0b%d" % d))
        load_bcast(w0b[d], e.w0[l, d:d + 1, :])
        load_bcast(a0b[d], e.a0[l, d:d + 1, :])
        wbs.append(sb("wbsB%d" % d, [64, HW], F32, st))
        abs_.append(sb("absB%d" % d, [64, HW], F32, st))
        k.dma("sp", wbs[d], e.w_b[l, d])
        k.dma("sp", abs_[d], e.a_b[l, d])

    zc = sb("zcB", [128, RC], F32, st)
    zp = sb("zpB", [128, RC], F32, st)
    zn = sb("znB", [128, RC], F32, st)
    kk0, kk, tA, tB, sig, av, kmod, bv = (t2(n) for n in ("kk0", "kk", "tA", "tB", "sig", "av", "kmod", "bv"))
    tots, e_in, e_ng, e_ex, e_rm, e_tot = (t2(n) for n in ("tots", "e_in", "e_ng", "e_ex", "e_rm", "e_tot"))
    ktl, rtl, kh, bh, khg, bhg = (t2(n) for n in ("ktl", "rtl", "kh", "bh", "khg", "bhg"))
    bg = sb("bgB", [128, 2 * HW], F32, st)
    ssk = sb("sskB", [128, HL], F32, st)
    sskt = sb("ssktB", [128, HL], F32, st)
    rkk = sb("rkkB", [128, HL], F32, st)
    txw = sb("txwB", [128, 64], F32, st)
    sxg = sb("sxgB", [128, 128], F32, st)
    txwT = sb("txwTB", [64, 128], F32, st)
    xaT = sb("xaTB", [64, 128], F32, st)
    sxgT = sb("sxgTB", [128, 128], F32, st)
    KR = sb("KRB", [64, HL, 2, 128], F32, st)
    khT = sb("khTB", [64, HL, 128], F32, st)
    bhT = sb("bhTB", [64, HL, 128], F32, st)
    AkT = sb("AkTB", [128, HL, 128], F32, st)
    BkT = sb("BkTB", [128, HL, 128], F32, st)
    BbT = sb("BbTB", [128, HL, 128], F32, st)
    Xin = sb("XinB", [128, HL, 128], F32, st)
    WXs = sb("WXsB", [128, HL, 128], F32, st)
    nX0 = sb("nX0B", [128, HL, 64], F32, st)
    ArT = sb("ArTB", [128, HL, 128], F32, st)
    GI = 2
    bufsets = []
    for sl in range(2):
        bufsets.append(dict(
            Ar=sb("ArB%d" % sl, [128, GI, 128], F32, st),
            Pm=[sb("PmB%d_%d" % (sl, i), [128, GI, 128], F32, st) for i in range(2)],
            Nm=[sb("NmB%d_%d" % (sl, i), [128, GI, 128], F32, st) for i in range(2)],
            Dm=[sb("DmB%d_%d" % (sl, i), [128, GI, 128], F32, st) for i in range(2)],
            DTm=[sb("DTmB%d_%d" % (sl, i), [128, GI, 128], F32, st) for i in range(2)]))
    HM = e.HM
    dgam = sb("dgamB", [64, HL, 64], F32, st)
    TTs = sb("TTsB", [64, HW], F32, st)
    ZTs = sb("ZTsB", [64, HL, 128], F32, st)
    Gs = sb("GsB", [64, HW], F32, st)
    Y0s = t2("Y0s")
    Hs = [sb("HsB%d" % i, [64, HW], F32, st) for i in range(2)]
    ysb = [t2("ysb%d" % i) for i in range(2)]

    def g3(v):
        return v.re("p (h d) -> p h d", d=64)

    def hs(v, h):
        return v[:, h * 64:(h + 1) * 64]

    cnt = [0]

    def ev():
        cnt[0] += 1
        return "act" if cnt[0] % 2 else "dve"

    for d in range(2):
        order = list(range(NT)) if d == 0 else (list(range(NCT - 1, -1, -1)) + list(range(NT - 1, NCT - 1, -1)))
        Ms, Mi, MsT, Tri = (MU, MUI, ML, MUI) if d == 0 else (ML, MLI, MU, MLI)
        k.memset("dve", Hs[0], 0.0)
        cur = 0
        for ti in order:
            r0 = zrow(ti)
            k.dma("sp", zc, e.zr[r0:r0 + 128, :])
            k.dma("sp", zp, e.zr[r0 - 1:r0 + 127, :])
            k.dma("sp", zn, e.zr[r0 + 1:r0 + 129, :])
            k.tt("pool", zp, zp, zn, ALU.add)
            k.tt("pool", zp, zp, muh, ALU.mult)
            k.tt("pool", zc, zc, omu, ALU.mult)
            k.tt("pool", zc, zc, zp, ALU.add)
            rr, kx, vx = zc[:, 0:HW], zc[:, HW:2 * HW], zc[:, 2 * HW:3 * HW]
            xw = zc[:, 3 * HW:3 * HW + 64]
            xa = zc[:, 3 * HW + 64:3 * HW + 128]
            xg = zc[:, 3 * HW + 128:3 * HW + 256]
            k.tt("dve", kk0, kx, kkb, ALU.mult)
            k.tt("pool", tA, kk0, kk0, ALU.mult)
            k.red(ssk, g3(tA), ALU.add)
            k.act(sskt, ssk, AF.Sqrt)
            k.ts("dve", sskt, sskt, 1e-12, None, ALU.max)
            k.recip(rkk, sskt)
            k.tt("dve", g3(kk), g3(kk0), rkk.un(2).bc([128, HL, 64]), ALU.mult)
            k.act(txw, xw, AF.Tanh)
            p = ps()
            k.tr(p[0:64, 0:128], txw, ident)
            k.tr(p[0:64, 128:256], xa, ident)
            k.cp("act", txwT, p[0:64, 0:128])
            k.cp("dve", xaT, p[0:64, 128:256])
            if d == 0:
                k.act(sxg, xg, AF.Sigmoid)
                p = ps()
                k.tr(p[:, 0:128], sxg, ident)
                k.cp("act", sxgT, p[:, 0:128])
                p = ps()
                k.mm(p[:, 0:HW], sxgT, gbs)
                k.cp("act", bg[:, HW:2 * HW], p[:, 0:HW])
                k.tt("pool", tA, rr, kx, ALU.mult)
                k.tt("pool", tA, tA, rkb, ALU.mult)
                k.red(ssk, g3(tA), ALU.add)
                k.tt("dve", g3(bg[:, 0:HW]), g3(vx), ssk.un(2).bc([128, HL, 64]), ALU.mult)
                k.dma("pool", e.bgd[ti * 128:(ti + 1) * 128, :], bg)
            p = ps()
            k.mm(p[:, 0:HW], txwT, wbs[d])
            k.tt("dve", tA, p[:, 0:HW], w0b[d], ALU.add)
            k.act(sig, tA, AF.Sigmoid)
            p = ps()
            k.mm(p[:, 0:HW], xaT, abs_[d])
            k.tt("dve", tB, p[:, 0:HW], a0b[d], ALU.add)
            k.act(av, tB, AF.Sigmoid)
            k.tt("pool", tB, av, kab, ALU.mult)
            k.tt("pool", tB, tB, omka, ALU.add)
            k.tt("pool", kmod, kx, tB, ALU.mult)
            k.tt("pool", bv, kk, av, ALU.mult)
            if BSTOP <= 1:
                continue
            pc = ps()
            k.mm(pc[:, 0:HW], Tri, sig)
            pt = ps()
            k.mm(pt[:, 0:HW], ones, sig)
            k.cp("dve", kk0, pc[:, 0:HW])
            k.cp("dve", tots, pt[:, 0:HW])
            k.act(e_in, kk0, AF.Exp, scale=-C0)
            k.act(e_ng, kk0, AF.Exp, scale=C0)
            k.tt("pool", tA, kk0, sig, ALU.subtract)
            k.act(e_ex, tA, AF.Exp, scale=-C0)
            k.tt("pool", tB, kk0, tots, ALU.subtract)
            k.act(e_rm, tB, AF.Exp, scale=C0)
            k.act(e_tot, tots, AF.Exp, scale=-C0)
            if BSTOP <= 1.2:
                continue
            k.tt("dve", ktl, kk, e_ex, ALU.mult)
            k.tt("pool", rtl, rr, e_in, ALU.mult)
            k.tt("dve", kh, kmod, e_ng, ALU.mult)
            k.tt("pool", bh, bv, e_ng, ALU.mult)
            k.tt("dve", khg, kmod, e_rm, ALU.mult)
            k.tt("pool", bhg, bv, e_rm, ALU.mult)
            if BSTOP <= 1.5:
                continue
            k.tt("dve", dgam, g3(e_tot[0:64, :]), ident[0:64, 0:64].un(1).bc([64, HL, 64]), ALU.mult)
            if BSTOP <= 1.8:
                continue
            k.cp("act", Xin[:, :, 0:64], g3(ktl))
            if BSTOP <= 2:
                continue
            for (src, dst) in ((ktl, lambda h: KR[:, h, 0, :]), (rtl, lambda h: KR[:, h, 1, :]),
                               (kh, lambda h: khT[:, h, :]), (bh, lambda h: bhT[:, h, :])):
                for h0 in range(0, HL, 4):
                    p = ps()
                    for j in range(4):
                        k.tr(p[0:64, j * 128:(j + 1) * 128], hs(src, h0 + j), ident)
                    for j in range(4):
                        k.cp(ev(), dst(h0 + j), p[0:64, j * 128:(j + 1) * 128])
            if BSTOP <= 3:
                continue
            for h0 in range(0, HL, 2):
                p = ps()
                p2 = ps()
                for j in range(2):
                    h = h0 + j
                    k.mm(p[:, j * 256:(j + 1) * 256], khT[:, h, :], KR[:, h, :, :].re("p a t -> p (a t)"))
                    k.mm(p2[:, j * 256:(j + 1) * 256], bhT[:, h, :], KR[:, h, :, :].re("p a t -> p (a t)"))
                pv = p.re("p (j a t) -> p j a t", a=2, t=128)
                p2v = p2.re("p (j a t) -> p j a t", a=2, t=128)
                m2 = lambda m: m.un(1).bc([128, 2, 128])
                k.tt("dve", AkT[:, h0:h0 + 2, :], pv[:, :, 0, :], m2(Ms), ALU.mult)
                k.tt("dve", BkT[:, h0:h0 + 2, :], pv[:, :, 1, :], m2(Mi), ALU.mult)
                k.tt("dve", BbT[:, h0:h0 + 2, :], p2v[:, :, 1, :], m2(Mi), ALU.mult)
                k.cp("act", ArT[:, h0:h0 + 2, :], p2v[:, :, 0, :])
            m4 = lambda m: m.un(1).bc([128, 4, 128])
            p = ps()
            for h in range(HL):
                k.mm(p[:, h * 64:(h + 1) * 64], AkT[:, h, :], hs(vx, h))
            k.cp("act", Xin[:, :, 64:128], p[:, 0:HW].re("p (h d) -> p h d", d=64))
            if d == 0:
                m16_ts, m16_st = HM[0], HM[1]
                mE_ts, mE_st = (HM[2], HM[4], HM[6]), (HM[3], HM[5], HM[7])
            else:
                m16_ts, m16_st = HM[1], HM[0]
                mE_ts, mE_st = (HM[3], HM[5], HM[7]), (HM[2], HM[4], HM[6])

            def inv_group(h0, slot):
                B_ = bufsets[slot]
                Ar, Pm, Nm, Dm, DTm = B_["Ar"], B_["Pm"], B_["Nm"], B_["Dm"], B_["DTm"]
                mg = lambda m: m.un(1).bc([128, GI, 128])
                e1, e2 = ("act", "dve") if slot == 0 else ("dve", "act")

                def mmg(lhs, rhs):
                    p_ = ps()
                    for j in range(GI):
                        k.mm(p_[:, j * 128:(j + 1) * 128], lhs[:, j, :], rhs[:, j, :])
                    return p_[:, 0:GI * 128].re("p (j t) -> p j t", t=128)

                p = ps()
                for j in range(GI):
                    k.mm(p[:, j * 128:(j + 1) * 128], KR[:, h0 + j, 0, :], bhT[:, h0 + j, :])
                k.cp(e1, Ar, p[:, 0:GI * 128].re("p (j t) -> p j t", t=128))
                ArTg = ArT[:, h0:h0 + GI, :]
                yield
                k.stt(Nm[0], Ar, -1.0, mg(m16_ts), ALU.mult, ALU.mult)
                k.stt(Pm[0], ArTg, -1.0, mg(m16_st), ALU.mult, ALU.mult)
                k.tt("dve", Dm[0], Nm[0], mg(ident), ALU.add)
                k.tt("dve", DTm[0], Pm[0], mg(ident), ALU.add)
                yield
                c = 0
                dc = 0
                for lev in range(1, 4):
                    n_ = 1 - c
                    pp = mmg(Nm[c], Pm[c])
                    pn = mmg(Pm[c], Nm[c])
                    k.cp(e1, Pm[n_], pp)
                    k.cp(e2, Nm[n_], pn)
                    yield
                    pd = mmg(Pm[n_], Dm[dc])
                    pdt = mmg(Nm[n_], DTm[dc])
                    k.tt("dve", Dm[1 - dc], Dm[dc], pd, ALU.add)
                    k.tt("dve", DTm[1 - dc], DTm[dc], pdt, ALU.add)
                    yield
                    c = n_
                    dc = 1 - dc
                for lev in range(3):
                    Eb, ETb, Xb, Yb = Nm[0], Pm[0], Nm[1], Pm[1]
                    k.tt("dve", Eb, Ar, mg(mE_ts[lev]), ALU.mult)
                    if lev < 2:
                        k.tt("dve", ETb, ArTg, mg(mE_st[lev]), ALU.mult)
                    py = mmg(Eb, DTm[dc])
                    k.cp(e1, Yb, py)
                    if lev < 2:
                        px = mmg(ETb, Dm[dc])
                        k.cp(e2, Xb, px)
                    yield
                    pdt = mmg(Dm[dc], Yb)
                    k.tt("dve", DTm[1 - dc], DTm[dc], pdt, ALU.subtract)
                    if lev < 2:
                        pd = mmg(DTm[dc], Xb)
                        k.tt("dve", Dm[1 - dc], Dm[dc], pd, ALU.subtract)
                    yield
                    dc = 1 - dc
                MT = DTm[dc]
                p = ps()
                for j in range(GI):
                    k.mm(p[:, j * 128:(j + 1) * 128], MT[:, j, :], Xin[:, h0 + j, :])
                pv = p[:, 0:GI * 128].re("p (j t) -> p j t", t=128)
                k.cp(e1, WXs[:, h0:h0 + GI, :], pv)
                k.ts("dve", nX0[:, h0:h0 + GI, :], pv[:, :, 64:128], -1.0, None, ALU.mult)
                yield

            gens = [inv_group(h0, i % 2) for i, h0 in enumerate(range(0, HL, GI))]
            for a_ in range(0, len(gens), 2):
                active = gens[a_:a_ + 2]
                while active:
                    for g_ in list(active):
                        try:
                            next(g_)
                        except StopIteration:
                            active.remove(g_)
            if BSTOP <= 5:
                continue
            p = ps()
            for h in range(HL):
                k.mm(p[0:64, h * 64:(h + 1) * 64], WXs[:, h, 0:64], hs(bhg, h))
            k.tt("dve", TTs, dgam.re("p h d -> p (h d)"), p[0:64, 0:HW], ALU.subtract)
            for h0 in range(0, HL, 4):
                p = ps()
                for j in range(4):
                    h = h0 + j
                    k.mm(p[0:64, j * 128:(j + 1) * 128], WXs[:, h, 0:64], BbT[:, h, :])
                k.tt("dve", ZTs[:, h0:h0 + 4, :], KR[:, h0:h0 + 4, 1, :], p[0:64, :].re("p (j t) -> p j t", t=128),
                     ALU.subtract)
            p = ps()
            for h in range(HL):
                k.mm(p[0:64, h * 64:(h + 1) * 64], hs(khg, h), hs(vx, h), start=True, stop=False)
                k.mm(p[0:64, h * 64:(h + 1) * 64], hs(bhg, h), nX0[:, h, :], start=False, stop=True)
            k.cp("act", Gs, p[0:64, 0:HW])
            p = ps()
            for h in range(HL):
                k.mm(p[:, h * 64:(h + 1) * 64], BkT[:, h, :], hs(vx, h), start=True, stop=False)
                k.mm(p[:, h * 64:(h + 1) * 64], BbT[:, h, :], nX0[:, h, :], start=False, stop=True)
            k.cp("act", Y0s, p[:, 0:HW])
            if BSTOP <= 6:
                continue
            Hc, Hn = Hs[cur], Hs[1 - cur]
            pY = ps()
            for h in range(HL):
                k.mm(pY[:, h * 64:(h + 1) * 64], ZTs[:, h, :], hs(Hc, h))
            pH = ps()
            for h in range(HL):
                k.mm(pH[0:64, h * 64:(h + 1) * 64], hs(TTs, h), hs(Hc, h))
            k.tt("dve", Hn, pH[0:64, 0:HW], Gs, ALU.add)
            y = ysb[ti % 2]
            k.tt("dve", y, pY[:, 0:HW], Y0s, ALU.add)
            k.dma("pool", e.yd[d, ti * 128:(ti + 1) * 128, :], y)
            cur = 1 - cur
    S.barrier()
    yf, yr_, cen, sq = tA, tB, kk0, kk
    ob = [e_in.bitcast(BF16)[:, 0:HW], e_ng.bitcast(BF16)[:, 0:HW]]
    mean, var, vtmp = ssk, sskt, rkk
    for ti in range(NT):
        if ti < NCT and not e.need_ctx:
            continue
        k.dma("sp", yf, e.yd[0, ti * 128:(ti + 1) * 128, :])
        k.dma("sp", yr_, e.yd[1, ti * 128:(ti + 1) * 128, :])
        k.dma("sp", bg, e.bgd[ti * 128:(ti + 1) * 128, :])
        k.tt("dve", yf, yf, yr_, ALU.add)
        k.red(mean, g3(yf), ALU.add)
        k.ts("dve", mean, mean, 1.0 / 64, None, ALU.mult)
        k.tt("dve", g3(cen), g3(yf), mean.un(2).bc([128, HL, 64]), ALU.subtract)
        k.tt("pool", sq, cen, cen, ALU.mult)
        k.red(var, g3(sq), ALU.add)
        rstd_of(var, 64, 64e-5, vtmp)
        k.tt("dve", g3(cen), g3(cen), var.un(2).bc([128, HL, 64]), ALU.mult)
        k.tt("pool", cen, cen, lngb, ALU.mult)
        k.tt("pool", cen, cen, lnbb, ALU.add)
        k.tt("pool", cen, cen, bg[:, 0:HW], ALU.add)
        o_ = ob[ti % 2]
        k.tt("dve", o_, cen, bg[:, HW:2 * HW], ALU.mult)
        k.dma("pool", e.omix[ti * 128:(ti + 1) * 128, 0:HW], o_)


def phase_C(nc, S, k, sb, ps, st, l, env):
    e = E(env)
    HL, HW, T, NT, NCT = e.HL, e.HW, e.T, e.NT, e.NCT
    psb = e.psb
    scale = 32.0 ** -0.5
    negB = sb("negBC", [128, 2 * HL], F32, st)
    e.load_bcast(negB, e.nbd[0:1, :])
    lv = {}
    for n in ("lam_q1", "lam_k1", "lam_q2", "lam_k2"):
        lv[n] = sb(n + "C", [128, 32], F32, st)
        e.load_bcast(lv[n], e.lamv[n][l:l + 1, :])
    ltmp = sb("ltmpC", [128, 32], F32, st)
    l1 = sb("l1C", [128, 1], F32, st)
    l2 = sb("l2C", [128, 1], F32, st)
    nlam = sb("nlamC", [128, 1], F32, st)
    k.tt("dve", ltmp, lv["lam_q1"], lv["lam_k1"], ALU.mult)
    k.red(l1, ltmp, ALU.add)
    k.act(l1, l1, AF.Exp)
    k.tt("dve", ltmp, lv["lam_q2"], lv["lam_k2"], ALU.mult)
    k.red(l2, ltmp, ALU.add)
    k.act(l2, l2, AF.Exp)
    k.tt("dve", nlam, l2, l1, ALU.subtract)
    k.ts("dve", nlam, nlam, -e.lam_init, None, ALU.add)
    sg = sb("sgC", [128, 64], F32, st)
    e.load_bcast(sg, e.subln_g[l:l + 1, :])
    k.ts("dve", sg, sg, 1.0 - e.lam_init, None, ALU.mult)
    vsb = sb("vsbC", [128, NT, HL * 65], BF16, st)
    k.dma("sp", vsb, e.vv.rearrange("(kt p) c -> p kt c", p=128))
    QT = [sb("QTC%d" % i, [64, T], BF16, st) for i in range(2)]
    KT = [sb("KTC%d" % i, [64, T], BF16, st) for i in range(2)]
    pT = [sb("pTC%d" % i, [128, 512], BF16, st) for i in range(3)]
    o1 = [sb("o1C%d" % i, [128, 65], F32, st) for i in range(4)]
    ot = [sb("otC%d" % i, [128, 64], F32, st) for i in range(2)]
    osq = sb("osqC", [128, 64], F32, st)
    odb = [sb("odbC%d" % i, [128, 64], BF16, st) for i in range(4)]
    r1 = sb("r1C", [128, 1], F32, st)
    r2 = sb("r2C", [128, 1], F32, st)
    ssq = sb("ssqC", [128, 1], F32, st)
    stmp = sb("stmpC", [128, 1], F32, st)
    sc_banks = psb[0:4]
    acc = psb[4:8]
    it = 0
    oi = 0
    chunks = []
    for c0 in range(NCT, NT, 4):
        chunks.append((list(range(c0, min(NT, c0 + 4))), list(range(NT))))
    if e.need_ctx:
        chunks.append((list(range(NCT)), list(range(NCT))))
    for h in range(HL):
        qt, kt_ = QT[h % 2], KT[h % 2]
        k.dma("sp", qt, e.qkT[h * 64:(h + 1) * 64, :])
        k.dma("sp", kt_, e.qkT[HW + h * 64:HW + (h + 1) * 64, :])
        for (qtiles, ktiles) in chunks:
            q0 = qtiles[0] * 128
            nq = len(qtiles) * 128
            for s in range(2):
                def issue_qk(kt, it_):
                    p_ = sc_banks[it_ % 4]
                    k.mm(p_[:, 0:nq], kt_[s * 32:(s + 1) * 32, kt * 128:(kt + 1) * 128],
                         qt[s * 32:(s + 1) * 32, q0:q0 + nq])
                    return p_

                pend = issue_qk(ktiles[0], it)
                for ki, kt in enumerate(ktiles):
                    p = pend
                    if ki + 1 < len(ktiles):
                        pend = issue_qk(ktiles[ki + 1], it + 1)
                    pt = pT[it % 3]
                    it += 1
                    k.act(pt[:, 0:nq], p[:, 0:nq], AF.Exp, scale=scale, bias=negB[:, 2 * h + s:2 * h + s + 1])
                    for j in range(len(qtiles)):
                        k.mm(acc[j][:, 0:65], pt[:, j * 128:(j + 1) * 128], vsb[:, kt, h * 65:(h + 1) * 65],
                             start=(ki == 0), stop=(ki == len(ktiles) - 1))
                if s == 0:
                    for j in range(len(qtiles)):
                        k.cp("act", o1[j], acc[j][:, 0:65])
                else:
                    for j, qti in enumerate(qtiles):
                        o_ = ot[oi % 2]
                        ob_ = odb[oi % 4]
                        oi += 1
                        k.recip(r1, o1[j][:, 64:65])
                        k.recip(r2, acc[j][:, 64:65])
                        k.tt("dve", r2, r2, nlam, ALU.mult)
                        k.ts("dve", o_, o1[j][:, 0:64], r1[:, 0:1], None, ALU.mult)
                        k.stt(o_, acc[j][:, 0:64], r2[:, 0:1], o_, ALU.mult, ALU.add)
                        k.tt("dve", osq, o_, o_, ALU.mult)
                        k.red(ssq, osq, ALU.add)
                        e.rstd_of(ssq, 64, 1e-5, stmp)
                        k.stt(ob_, o_, ssq[:, 0:1], sg, ALU.mult, ALU.mult)
                        k.dma("pool", e.omix[qti * 128:(qti + 1) * 128, HW + h * 64:HW + (h + 1) * 64], ob_)


def _consts(L):
    T = CTX + L
    ident = np.eye(128, dtype=np.float32)
    i = np.arange(128)
    U = i[:, None] < i[None, :]
    UI = i[:, None] <= i[None, :]
    LO = i[:, None] > i[None, :]
    LI = i[:, None] >= i[None, :]
    masks = np.concatenate([U, UI, LO, LI], axis=1).astype(np.float32)
    ii, jj = i[:, None], i[None, :]
    L16 = (ii // 16 == jj // 16) & (jj < ii)
    hm = [L16, L16.T]
    for b in (16, 32, 64):
        EL = (ii // (2 * b) == jj // (2 * b)) & (ii // b == jj // b + 1)
        hm += [EL, EL.T]
    masks2 = np.concatenate(hm, axis=1).astype(np.float32)
    t = np.arange(L)
    inv = 10000.0 ** (-np.arange(8, dtype=np.float64) / 8.0)
    ar = (t // GRID_W)[:, None] * inv[None, :]
    ac = (t % GRID_W)[:, None] * inv[None, :]
    cos32 = np.concatenate([np.cos(ar), np.cos(ar), np.cos(ac), np.cos(ac)], axis=1)
    sin32 = np.concatenate([-np.sin(ar), np.sin(ar), -np.sin(ac), np.sin(ac)], axis=1)
    rope = np.zeros((T, 64), np.float32)
    rope[:CTX, 0:32] = 1.0
    rope[CTX:, 0:32] = cos32
    rope[CTX:, 32:64] = sin32
    return ident, masks, masks2, rope


_CACHE = {}


def run(cfg, inputs, ncores=8):
    L = cfg["L"]
    DEPTH = cfg["DEPTH"]
    key = (L, DEPTH, tuple(cfg.get("dbg", ())), cfg.get("phases", "MABCDE"))
    if key not in _CACHE:
        _CACHE[key] = build(cfg)
    nc, S = _CACHE[key]
    f = lambda a: np.ascontiguousarray(np.asarray(a, dtype=np.float32))
    ident, masks, masks2, rope = _consts(L)
    B = inputs["x"].shape[0]
    shared = {n: f(inputs[n]) for n in (
        "ada_w", "ada_b", "g_pre_mix", "g_post_mix", "g_pre_mlp", "g_post_mlp", "w_in", "shift_mu", "k_k", "k_a",
        "w0", "w_b", "a0", "a_b", "g_b", "lnx_g", "lnx_b", "lam_q1", "lam_k1", "lam_q2", "lam_k2", "subln_g",
        "w_out", "w_ff1", "w_ff2")}
    shared["r_k"] = f(inputs["r_k"]).reshape(DEPTH, 512)
    shared["c_ident"] = ident
    shared["c_masks"] = masks
    shared["c_masks2"] = masks2
    shared["c_rope"] = rope
    in_maps = []
    for c in range(ncores):
        b = c % B
        m = dict(shared)
        m["xin"] = np.ascontiguousarray(np.concatenate([f(inputs["ctx"][b]), f(inputs["x"][b])], axis=0))
        m["c2"] = np.ascontiguousarray(np.stack([f(inputs["c"][b]), f(inputs["c_ctx"])], axis=0))
        in_maps.append(m)
    res = run_bass_kernel_spmd(nc, in_maps, core_ids=list(range(ncores)))
    return res.results


def kernel(**inputs):
    cfg = dict(L=8192, DEPTH=4)
    results = run(cfg, inputs, 8)
    B = inputs["x"].shape[0]
    return np.stack([np.asarray(results[b]["out"], dtype=np.float32) for b in range(B)], axis=0)
```
